# Optimizing a Trainium2 kernel written in Bass

```python
import math
import jax, jax.numpy as jnp
from jax import lax
import numpy as np

D_MODEL = 1024
BATCH = 8
SEQ = 2048
DEPTH = 2

CHUNK = 64
Q_BLOCK = 128
N_BRANCH = 3
BRANCH_WIDTH = D_MODEL // 2
ATT_HEADS = 4
ATT_QK_DIM = BRANCH_WIDTH // (2 * ATT_HEADS)
ATT_V_DIM = 2 * ATT_QK_DIM
ATT_QK_WIDTH = ATT_HEADS * 2 * ATT_QK_DIM
POOL_GROUPS = 4
POOL_WINDOWS = (2, 4, 8, 16)
POOL_GROUP_DIM = BRANCH_WIDTH // POOL_GROUPS
SGU_GROUPS = 4
SGU_BLOCK = 128
SGU_GROUP_DIM = BRANCH_WIDTH // SGU_GROUPS
IN_SPLIT_SIZES = (ATT_QK_WIDTH, ATT_QK_WIDTH, BRANCH_WIDTH, BRANCH_WIDTH,
                  BRANCH_WIDTH, BRANCH_WIDTH,
                  BRANCH_WIDTH, BRANCH_WIDTH, BRANCH_WIDTH)
IN_WIDTH = sum(IN_SPLIT_SIZES)
NORM_EPS = 1e-6

kernel_name = "hybrid_diffattn_pool_sgu_block"


def rms_norm(x, g):
    xf = x.astype(jnp.float32)
    y = xf * lax.rsqrt(jnp.mean(xf * xf, axis=-1, keepdims=True) + NORM_EPS)
    return (y * g.astype(jnp.float32)).astype(x.dtype)


def layer_norm(x, g, b):
    xf = x.astype(jnp.float32)
    mu = jnp.mean(xf, axis=-1, keepdims=True)
    var = jnp.mean(jnp.square(xf - mu), axis=-1, keepdims=True)
    y = (xf - mu) * lax.rsqrt(var + NORM_EPS)
    return (y * g.astype(jnp.float32) + b.astype(jnp.float32)).astype(x.dtype)


def diff_attention(q, k, v, lam):
    B, H, _, S, _ = q.shape
    key_chunk = jnp.arange(S) // CHUNK
    scale = ATT_QK_DIM ** -0.5

    def attend_block(start):
        qb = lax.dynamic_slice_in_dim(q, start, Q_BLOCK, axis=3)
        q_chunk = (start + jnp.arange(Q_BLOCK)) // CHUNK
        mask = key_chunk[None, :] <= q_chunk[:, None]
        s = jnp.einsum('bhmqd,bhmkd->bhmqk', qb, k).astype(jnp.float32) * scale
        s = jnp.where(mask, s, -jnp.inf)
        p = jax.nn.softmax(s, axis=-1)
        a = p[:, :, 0] - lam * p[:, :, 1]
        return jnp.einsum('bhqk,bhkd->bhqd', a.astype(v.dtype), v)

    starts = jnp.arange(S // Q_BLOCK) * Q_BLOCK
    o = lax.map(attend_block, starts)
    return o.transpose(1, 2, 0, 3, 4).reshape(B, H, S, ATT_V_DIM)


def multi_scale_pool(u, w, b, scale):
    B, S, _ = u.shape
    ug = u.reshape(B, S, POOL_GROUPS, POOL_GROUP_DIM)
    uf = ug.astype(jnp.float32)
    cs = jnp.pad(jnp.cumsum(uf, axis=1), ((0, 0), (1, 0), (0, 0), (0, 0)))
    t = jnp.arange(S)[:, None]
    win = jnp.array(POOL_WINDOWS, dtype=jnp.int32)[None, :]
    lo = jnp.maximum(t + 1 - win, 0)
    window_sum = cs[:, 1:] - cs[:, lo, jnp.arange(POOL_GROUPS)[None, :]]
    count = (t + 1 - lo).astype(jnp.float32)[None, :, :, None]
    pooled = (window_sum / count - uf).astype(u.dtype)
    mixed = jnp.einsum('bsgc,gcd->bsgd', pooled, w) + b
    return mixed.reshape(B, S, BRANCH_WIDTH) * scale


def spatial_gating(u, v, ln_g, ln_b, w_s, b_s):
    B, S, _ = u.shape
    nb = S // SGU_BLOCK
    u = jax.nn.gelu(u, approximate=False)
    v = layer_norm(jax.nn.gelu(v, approximate=False), ln_g, ln_b)
    vb = v.reshape(B, nb, SGU_BLOCK, SGU_GROUPS, SGU_GROUP_DIM)
    pos_chunk = jnp.arange(SGU_BLOCK) // CHUNK
    mask = pos_chunk[None, :] <= pos_chunk[:, None]
    w = jnp.where(mask[None], w_s, jnp.zeros_like(w_s))
    mixed = jnp.einsum('gij,bnjgc->bnigc', w, vb) + b_s.T[None, None, :, :, None]
    ub = u.reshape(B, nb, SGU_BLOCK, SGU_GROUPS, SGU_GROUP_DIM)
    return (ub * mixed).reshape(B, S, BRANCH_WIDTH)


def hybrid_layer(x, layer_idx, pre_g, post_g, w_in, lq1, lk1, lq2, lk2, subln_g,
                 pool_w, pool_b, pool_scale, sgu_ln_g, sgu_ln_b, sgu_w, sgu_b,
                 w_branch, w_merge, b_merge, w_out):
    B, S, D = x.shape
    h = rms_norm(x, pre_g)
    z = h @ w_in
    offsets = []
    acc = 0
    for size in IN_SPLIT_SIZES[:-1]:
        acc += size
        offsets.append(acc)
    q, k, v, g_a, p_in, g_b, s_u, s_v, g_c = jnp.split(z, offsets, axis=-1)

    q = q.reshape(B, S, ATT_HEADS, 2, ATT_QK_DIM).transpose(0, 2, 3, 1, 4)
    k = k.reshape(B, S, ATT_HEADS, 2, ATT_QK_DIM).transpose(0, 2, 3, 1, 4)
    v = v.reshape(B, S, ATT_HEADS, ATT_V_DIM).transpose(0, 2, 1, 3)
    lam_init = 0.8 - 0.6 * math.exp(-0.3 * layer_idx)
    f32 = jnp.float32
    lam = (jnp.exp(jnp.sum(lq1.astype(f32) * lk1.astype(f32)))
           - jnp.exp(jnp.sum(lq2.astype(f32) * lk2.astype(f32))) + lam_init)
    o = diff_attention(q, k, v, lam)
    o = rms_norm(o, subln_g) * (1.0 - lam_init)
    y_a = o.transpose(0, 2, 1, 3).reshape(B, S, BRANCH_WIDTH)

    y_b = multi_scale_pool(p_in, pool_w, pool_b, pool_scale)

    y_c = spatial_gating(s_u, s_v, sgu_ln_g, sgu_ln_b, sgu_w, sgu_b)

    ys = jnp.stack([y_a, y_b, y_c], axis=2) * jax.nn.silu(jnp.stack([g_a, g_b, g_c], axis=2))
    branch = jnp.einsum('bsnw,nwd->bsnd', ys, w_branch)
    gates = jax.nn.sigmoid(h @ w_merge + b_merge).reshape(B, S, N_BRANCH, D)
    merged = jnp.sum(gates * branch, axis=2)
    out = merged @ w_out
    return x + rms_norm(out, post_g)


def setup_inputs(seed: int = 0) -> dict:
    key = jax.random.key(seed)
    ks = jax.random.split(key, 24)
    n = jax.random.normal
    L, D, W = DEPTH, D_MODEL, BRANCH_WIDTH
    return {
        "x": n(ks[0], (BATCH, SEQ, D), jnp.float32),
        "pre_norm_g": 1.0 + 0.05 * n(ks[1], (L, D), jnp.float32),
        "post_norm_g": 1.0 + 0.05 * n(ks[2], (L, D), jnp.float32),
        "w_in": n(ks[3], (L, D, IN_WIDTH), jnp.float32) * D ** -0.5,
        "lambda_q1": 0.1 * n(ks[4], (L, ATT_QK_DIM), jnp.float32),
        "lambda_k1": 0.1 * n(ks[5], (L, ATT_QK_DIM), jnp.float32),
        "lambda_q2": 0.1 * n(ks[6], (L, ATT_QK_DIM), jnp.float32),
        "lambda_k2": 0.1 * n(ks[7], (L, ATT_QK_DIM), jnp.float32),
        "attn_subln_g": 1.0 + 0.05 * n(ks[8], (L, ATT_V_DIM), jnp.float32),
        "pool_w": n(ks[9], (L, POOL_GROUPS, POOL_GROUP_DIM, POOL_GROUP_DIM), jnp.float32) * POOL_GROUP_DIM ** -0.5,
        "pool_b": 0.02 * n(ks[10], (L, POOL_GROUPS, POOL_GROUP_DIM), jnp.float32),
        "pool_scale": 1.0 + 0.1 * n(ks[11], (L, W), jnp.float32),
        "sgu_ln_g": 1.0 + 0.05 * n(ks[12], (L, W), jnp.float32),
        "sgu_ln_b": 0.02 * n(ks[13], (L, W), jnp.float32),
        "sgu_w": n(ks[14], (L, SGU_GROUPS, SGU_BLOCK, SGU_BLOCK), jnp.float32) * SGU_BLOCK ** -0.5,
        "sgu_b": 1.0 + 0.05 * n(ks[15], (L, SGU_GROUPS, SGU_BLOCK), jnp.float32),
        "w_branch": n(ks[16], (L, N_BRANCH, W, D), jnp.float32) * W ** -0.5,
        "w_merge": n(ks[17], (L, D, N_BRANCH * D), jnp.float32) * D ** -0.5,
        "b_merge": 0.02 * n(ks[18], (L, N_BRANCH * D), jnp.float32),
        "w_out": n(ks[19], (L, D, D), jnp.float32) * D ** -0.5,
    }


def reference(x, pre_norm_g, post_norm_g, w_in, lambda_q1, lambda_k1, lambda_q2, lambda_k2,
              attn_subln_g, pool_w, pool_b, pool_scale, sgu_ln_g, sgu_ln_b, sgu_w, sgu_b,
              w_branch, w_merge, b_merge, w_out):
    for l in range(DEPTH):
        x = hybrid_layer(x, l, pre_norm_g[l], post_norm_g[l], w_in[l],
                         lambda_q1[l], lambda_k1[l], lambda_q2[l], lambda_k2[l], attn_subln_g[l],
                         pool_w[l], pool_b[l], pool_scale[l], sgu_ln_g[l], sgu_ln_b[l],
                         sgu_w[l], sgu_b[l], w_branch[l], w_merge[l], b_merge[l], w_out[l])
    return x
```

```python
import math
from contextlib import ExitStack
import numpy as np
import concourse.bass as bass
import concourse.mybir as mybir
from concourse.bass_utils import run_bass_kernel_spmd

F32 = mybir.dt.float32
BF16 = mybir.dt.bfloat16
AF = mybir.ActivationFunctionType
ALU = mybir.AluOpType

L = 2
T = 2048
D = 1024
NT = 16
EPS = 1e-6
NS = 3
SLOT = 4608
NPP = 33


class Region:
    __slots__ = ("w", "rs")

    def __init__(self):
        self.w = None
        self.rs = {}


class Sched:
    def __init__(self, nc):
        self.nc = nc
        self.E = {"pe": nc.tensor, "act": nc.scalar, "dve": nc.vector, "pool": nc.gpsimd, "sp": nc.sync}
        self.sem = {k: nc.alloc_semaphore("s_" + k) for k in self.E}
        self.cnt = {k: 0 for k in self.E}
        self.seen = {k: {} for k in self.E}
        self.regs = {}
        self.dsem = {}
        self.dcnt = {}
        self.pending = {k: {} for k in self.E}

    def reg(self, name):
        r = self.regs.get(name)
        if r is None:
            r = self.regs[name] = Region()
        return r

    def _R(self, xs):
        return [self.reg(x) if isinstance(x, str) else x for x in xs]

    def _deps(self, reads, writes):
        deps = {}
        for r in reads:
            if r.w is not None:
                k, v = r.w
                deps[k] = max(deps.get(k, 0), v)
        for w in writes:
            if w.w is not None:
                k, v = w.w
                deps[k] = max(deps.get(k, 0), v)
            for k, v in w.rs.items():
                deps[k] = max(deps.get(k, 0), v)
        return deps

    def _wait(self, eng, deps):
        for k, v in deps.items():
            if k == "pe" and eng == "pe":
                continue
            if self.seen[eng].get(k, 0) >= v:
                continue
            s = self.sem[k] if k in self.sem else self.dsem[k]
            self.E[eng].wait_ge(s, v)
            self.seen[eng][k] = v

    def _commit(self, t, reads, writes):
        k, v = t
        for r in reads:
            r.rs[k] = max(r.rs.get(k, 0), v)
        for w in writes:
            w.w = t
            w.rs = {}

    def _apply_pending(self, eng):
        p = self.pending[eng]
        if p:
            self.pending[eng] = {}
            self._wait(eng, p)

    def op(self, eng, fn, reads=(), writes=()):
        reads = self._R(reads)
        writes = self._R(writes)
        if eng != "pe":
            self._apply_pending(eng)
        self._wait(eng, self._deps(reads, writes))
        ins = fn()
        self.cnt[eng] += 1
        ins.then_inc(self.sem[eng], 1)
        t = (eng, self.cnt[eng])
        self._commit(t, reads, writes)
        return t

    def dma(self, q, semname, out, in_, reads=(), writes=()):
        reads = self._R(reads)
        writes = self._R(writes)
        if semname not in self.dsem:
            self.dsem[semname] = self.nc.alloc_semaphore("d_" + semname)
            self.dcnt[semname] = 0
        self._apply_pending(q)
        self._wait(q, self._deps(reads, writes))
        self.E[q].dma_start(out=out, in_=in_).then_inc(self.dsem[semname], 16)
        self.dcnt[semname] += 16
        t = (semname, self.dcnt[semname])
        self._commit(t, reads, writes)
        return t

    def barrier(self, hard=False):
        deps = {k: self.cnt[k] for k in ("pe", "act", "dve", "pool") if self.cnt[k] > 0}
        if hard:
            for k, v in self.dcnt.items():
                deps[k] = v
        for e in ("act", "dve", "pool", "sp"):
            p = self.pending[e]
            for k, v in deps.items():
                p[k] = max(p.get(k, 0), v)
        if hard:
            self._wait("pe", deps)
            for e in ("act", "dve", "pool", "sp"):
                self._apply_pending(e)


class _Stop(Exception):
    pass


def build(n_layers=L, stop=None):
    nc = bass.Bass("TRN2", target_bir_lowering=False)
    S = Sched(nc)
    reg = S.reg
    pe, act, dve, pool = nc.tensor, nc.scalar, nc.vector, nc.gpsimd

    def din(name, shape):
        return nc.dram_tensor(name, shape, F32, kind="ExternalInput").ap()

    x_d = din("x", [T, D])
    w_in_d = din("w_in_p", [L, 128, 8, 4608])
    w_c_d = din("w_c", [L, 8, 128, SLOT])
    w_out_d = din("w_out_p", [L, 128, 8, 1024])
    pool_w_d = din("pool_w_p", [L, 128, 4, 128])
    sgu_w_d = din("sgu_wT", [L, 128, 4, 128])
    vec_pp_d = din("vec_pp", [128, L * NPP])
    pre_g_d = din("pre_g_rep", [128, L, 1024])
    post_g_d = din("post_g_rep", [128, L, 1024])
    ln_g_d = din("ln_g_rep", [128, L, 512])
    ln_b_d = din("ln_b_rep", [128, L, 512])
    sgu_b_d = din("sgu_b_rep", [128, L, 512])
    lam_d = din("lam_rep", [128, L, 4, 64])
    ident_d = din("ident", [128, 128])
    rcnt_d = din("rcnt", [128, 4, 16])
    mask_a_d = din("mask_a", [1, 128])
    mask_b_d = din("mask_b", [1, 320])
    y_d = nc.dram_tensor("y", [T, D], F32, kind="ExternalOutput").ap()

    X = nc.alloc_sbuf_tensor("X", [128, NT, D], F32)
    HT = nc.alloc_sbuf_tensor("HT", [128, 8, T], BF16)
    YA = nc.alloc_sbuf_tensor("YA", [128, 4, T], BF16)
    RING = nc.alloc_sbuf_tensor("RING", [128, NS, SLOT], BF16)
    GREP = nc.alloc_sbuf_tensor("GREP", [128, D], F32)
    IDENT = nc.alloc_sbuf_tensor("IDENT", [128, 128], BF16)
    VPP = nc.alloc_sbuf_tensor("VPP", [128, L * NPP], F32)
    PW = nc.alloc_sbuf_tensor("PW", [128, 4, 128], BF16)
    SW = nc.alloc_sbuf_tensor("SW", [128, 4, 128], BF16)
    RCNT = nc.alloc_sbuf_tensor("RCNT", [128, 4, 16], F32)
    ST = nc.alloc_sbuf_tensor("ST", [128, 64], F32)
    ST2 = nc.alloc_sbuf_tensor("ST2", [128, 64], F32)
    MA = nc.alloc_sbuf_tensor("MA", [1, 128], BF16)
    MB = nc.alloc_sbuf_tensor("MB", [1, 320], BF16)
    PS = nc.alloc_psum_tensor("PS", [128, 8, 512], F32)

    def bank(b):
        return PS[:, b, :]

    def bankreg(b):
        return reg(f"ps{b}")

    blocks = []
    for l in range(n_layers):
        for kind in ["V", "GA", "QK0", "QK1", "QK2", "QK3", "PIN", "GB", "SV", "GC", "SU"]:
            blocks.append((l, kind))
        for hf in range(2):
            for c in range(8):
                blocks.append((l, f"C{hf}_{c}"))
            blocks.append((l, f"WO{hf}_0"))
            blocks.append((l, f"WO{hf}_1"))
    WIN_OFF = {"V": (0, 512), "GA": (512, 512), "QK0": (1024, 256), "QK1": (1280, 256), "QK2": (1536, 256),
               "QK3": (1792, 256), "GB": (2048, 512), "PIN": (2560, 512), "GC": (3072, 512), "SU": (3584, 512),
               "SV": (4096, 512)}
    state = {"next_load": 0, "next_use": 0, "released": 0}

    def issue_load(extra_reads=()):
        i = state["next_load"]
        if i >= len(blocks):
            return
        state["next_load"] += 1
        l, kind = blocks[i]
        s = i % NS
        slot = RING[:, s, :]
        if kind in WIN_OFF:
            c0, n = WIN_OFF[kind]
            out = slot[:, 0:8 * n].rearrange("p (k n) -> p k n", k=8)
            src = w_in_d[l, :, :, c0:c0 + n]
        elif kind.startswith("C"):
            c = int(kind.split("_")[1])
            out = slot
            src = w_c_d[l, c]
        else:
            hc = int(kind.split("_")[1])
            out = slot[:, 0:4096].rearrange("p (k n) -> p k n", k=8)
            src = w_out_d[l, :, :, hc * 512:(hc + 1) * 512]
        S.dma("pool", f"ring{s}", out, src, reads=list(extra_reads), writes=[f"ring{s}"])

    def next_block(expect):
        i = state["next_use"]
        state["next_use"] += 1
        assert blocks[i][1] == expect, (blocks[i], expect)
        assert state["next_load"] > i
        s = i % NS
        return RING[:, s, :], f"ring{s}"

    def release_block():
        state["released"] += 1
        while state["next_load"] < min(len(blocks), state["released"] + NS):
            issue_load()

    S.dma("sp", "vpp", VPP[:], vec_pp_d, writes=["vpp"])
    S.dma("sp", "rcnt", RCNT[:], rcnt_d, writes=["rcnt"])
    for t in range(NT):
        S.dma("sp", f"x{t // 4}", X[:, t, :], x_d[t * 128:(t + 1) * 128, :], writes=[f"x{t}"])
    for t in range(NT):
        reg(f"x{t}").w = (f"x{t // 4}", 64)
    S.dma("pool", "ident", IDENT[:], ident_d, writes=["ident"])
    S.dma("pool", "mska", MA[:], mask_a_d, writes=["mska"])
    S.dma("pool", "mskb", MB[:], mask_b_d, writes=["mskb"])
    issue_load(extra_reads=["x0"])
    for _ in range(NS - 1):
        issue_load(extra_reads=[f"x{t}" for t in range(NT)])

    def pp(l, j):
        return VPP[:, l * NPP + j: l * NPP + j + 1]

    open_stacks = []

    def chk(tag, **tens):
        if stop == tag:
            S.barrier(hard=True)
            for name, ap in tens.items():
                d = nc.dram_tensor("dbg_" + name, list(ap.shape), F32, kind="ExternalOutput").ap()
                S.dma("pool", "dbg", d, ap)
            nc.gpsimd.wait_ge(S.dsem["dbg"], S.dcnt["dbg"])
            while open_stacks:
                open_stacks.pop().close()
            raise _Stop()

    try:
      for l in range(n_layers):
          lam_init = 0.8 - 0.6 * math.exp(-0.3 * l)
          last = (l == n_layers - 1)
          S.dma("act" if l == 0 else "sp", "grep", GREP[:], pre_g_d[:, l, :], writes=["grep"])
          S.dma("pool", "pw", PW[:], pool_w_d[l], writes=["pw"])
          S.dma("pool", "sw", SW[:], sgu_w_d[l], writes=["sw"])
          S.op("dve", lambda: dve.tensor_scalar(out=ST[:, 9:10], in0=pp(l, 32), scalar1=(1.0 - lam_init),
                                                scalar2=None, op0=ALU.mult),
               reads=["vpp"], writes=["subln_s"])
          S.op("dve", lambda: dve.tensor_tensor(out=ST[:, 10:14], in0=VPP[:, l * NPP + 24:l * NPP + 28],
                                                in1=VPP[:, l * NPP + 28:l * NPP + 32], op=ALU.mult),
               reads=["vpp"], writes=["bps"])
          NEGLAM = ST[:, 8:9]
          SUBLN = ST[:, 9:10]

          def HTb(k, tb):
              return HT[:, k, tb * 512:(tb + 1) * 512]

          rr = {"b": 0}

          def proj_fm(W, wreg, col0, tb, evac, banks=(4, 5, 6, 7)):
              b = banks[rr["b"] % len(banks)]
              rr["b"] += 1

              def fn():
                  for k in range(8):
                      ins = pe.matmul(out=bank(b), lhsT=W[:, k, col0:col0 + 128], rhs=HTb(k, tb),
                                      start=(k == 0), stop=(k == 7))
                  return ins
              S.op("pe", fn, reads=[wreg, f"ht{tb}"], writes=[bankreg(b)])
              evac(b)

          def proj_tm(W, wreg, t, n, evac, banks=(4, 5, 6, 7)):
              b = banks[rr["b"] % len(banks)]
              rr["b"] += 1

              def fn():
                  for k in range(8):
                      ins = pe.matmul(out=PS[:, b, 0:n], lhsT=HT[:, k, t * 128:(t + 1) * 128], rhs=W[:, k, 0:n],
                                      start=(k == 0), stop=(k == 7))
                  return ins
              S.op("pe", fn, reads=[wreg, f"ht{t // 4}"], writes=[bankreg(b)])
              evac(b)

          with nc.sbuf_tensor(f"LAMV_{l}", [128, 4, 64], F32) as LAMV, \
                  nc.sbuf_tensor(f"VA_{l}", [128, NT, 4, 130], BF16) as VA, \
                  nc.sbuf_tensor(f"QK_{l}", [128, 2, 3, T], BF16) as QK:
              es = ExitStack()
              open_stacks.append(es)
              XS = es.enter_context(nc.sbuf_tensor(f"XS_{l}", [128, 2, D], BF16))
              JA = es.enter_context(nc.sbuf_tensor(f"JA_{l}", [128, D], BF16))
              S.dma("act" if l == 0 else "sp", "lamv", LAMV[:], lam_d[:, l], writes=["lamv"])
              S.op("dve", lambda: dve.scalar_tensor_tensor(out=LAMV[:, 0, :], in0=LAMV[:, 0, :], scalar=1.0, in1=LAMV[:, 1, :],
                                                            op0=ALU.mult, op1=ALU.mult, accum_out=ST[:, 0:1]),
                   reads=["lamv"], writes=["lamv", "st_lam0"])
              S.op("dve", lambda: dve.scalar_tensor_tensor(out=LAMV[:, 2, :], in0=LAMV[:, 2, :], scalar=1.0, in1=LAMV[:, 3, :],
                                                            op0=ALU.mult, op1=ALU.mult, accum_out=ST[:, 1:2]),
                   reads=["lamv"], writes=["lamv", "st_lam1"])
              S.op("act", lambda: act.activation(out=ST[:, 2:4], in_=ST[:, 0:2], func=AF.Exp),
                   reads=["st_lam0", "st_lam1"], writes=["st_lam2"])
              S.op("dve", lambda: dve.tensor_tensor(out=ST[:, 4:5], in0=ST[:, 3:4], in1=ST[:, 2:3], op=ALU.subtract),
                   reads=["st_lam2"], writes=["st_lam3"])
              S.op("dve", lambda: dve.tensor_scalar(out=ST[:, 8:9], in0=ST[:, 4:5], scalar1=-lam_init, scalar2=None,
                                                    op0=ALU.add),
                   reads=["st_lam3"], writes=["neglam"])
              S.op("dve", lambda: dve.memset(QK[64:128, :, 0, :], 0.0), writes=["qz0"])
              S.op("dve", lambda: dve.memset(QK[0:64, :, 1, :], 0.0), writes=["qz1"])
              W, wreg = next_block("V")
              Wv = W[:, 0:4096].rearrange("p (k n) -> p k n", k=8)
              S.op("dve", lambda: dve.memset(VA[:, :, :, 128:130], 1.0), writes=["va_ones"])

              def a_stats(g):
                  if l == 0:
                      for t in range(4 * g, 4 * g + 4):
                          S.op("act", lambda t=t: act.activation(out=JA[:], in_=X[:, t, :], func=AF.Square,
                                                                  accum_out=ST2[:, 16 + t:17 + t]),
                               reads=[f"x{t}"], writes=["ja", f"ss{t}"])
                      S.op("act", lambda: act.activation(out=ST2[:, 32 + 4 * g:36 + 4 * g],
                                                         in_=ST2[:, 16 + 4 * g:20 + 4 * g],
                                                         func=AF.Sqrt, scale=1.0 / D, bias=EPS),
                           reads=[f"ss{t}" for t in range(4 * g, 4 * g + 4)], writes=[f"sq{g}"])
                  else:
                      S.op("act", lambda: act.activation(out=ST2[:, 32 + 4 * g:36 + 4 * g],
                                                         in_=ST[:, 48 + 4 * g:52 + 4 * g],
                                                         func=AF.Sqrt, scale=1.0 / D, bias=EPS),
                           reads=[f"ssn{t}" for t in range(4 * g, 4 * g + 4)], writes=[f"sq{g}"])

              def a_recip(g):
                  S.op("dve", lambda: dve.reciprocal(out=ST2[:, 48 + 4 * g:52 + 4 * g], in_=ST2[:, 32 + 4 * g:36 + 4 * g]),
                       reads=[f"sq{g}"], writes=[f"rstd{g}"])

              def a_tile(t):
                  b = t % 2
                  S.op("dve", lambda: dve.scalar_tensor_tensor(
                      out=XS[:, b, :], in0=X[:, t, :], scalar=ST2[:, 48 + t:49 + t], in1=GREP[:],
                      op0=ALU.mult, op1=ALU.mult), reads=[f"x{t}", f"rstd{t // 4}", "grep"], writes=[f"xs{b}"])
                  pb = 6 + b
                  PT_ = PS[:, pb, :].bitcast(BF16).rearrange("p (k n) -> p k n", k=8)

                  def tr():
                      for k in range(8):
                          ins = pe.transpose(out=PT_[:, k, :], in_=XS[:, b, k * 128:(k + 1) * 128], identity=IDENT[:])
                      return ins
                  S.op("pe", tr, reads=[f"xs{b}", "ident"], writes=[bankreg(pb)])
                  S.op("act", lambda: act.copy(out=HT[:, :, t * 128:(t + 1) * 128], in_=PT_),
                       reads=[bankreg(pb)], writes=[f"ht{t // 4}"])

              def v_tile(t):
                  def ev(b):
                      S.op("dve", lambda: dve.tensor_copy(out=VA[:, t, :, 0:128],
                                                          in_=PS[:, b, :].rearrange("p (h d) -> p h d", h=4)),
                           reads=[bankreg(b)], writes=[f"va{t}"])
                  proj_tm(Wv, wreg, t, 512, ev, banks=(2, 3, 4, 5))
              if l == 0:
                  a_stats(0)
                  a_recip(0)
              else:
                  S.op("act", lambda: act.activation(out=ST2[:, 32:48], in_=ST[:, 48:64], func=AF.Sqrt,
                                                     scale=1.0 / D, bias=EPS),
                       reads=[f"ssn{t}" for t in range(NT)], writes=[f"sq{g}" for g in range(4)])
                  S.op("dve", lambda: dve.reciprocal(out=ST2[:, 48:64], in_=ST2[:, 32:48]),
                       reads=[f"sq{g}" for g in range(4)], writes=[f"rstd{g}" for g in range(4)])
              for g in range(4):
                  if l == 0 and g + 1 < 4:
                      a_stats(g + 1)
                  for t in range(4 * g, 4 * g + 4):
                      a_tile(t)
                  if l == 0 and g + 1 < 4:
                      a_recip(g + 1)
                  if g >= 1:
                      for t in range(4 * g - 4, 4 * g):
                          v_tile(t)
              for t in range(12, 16):
                  v_tile(t)
              release_block()
              chk("A" + str(l), HT=HT[:])
              S.dma("sp", "grep", GREP[:], post_g_d[:, l, :], writes=["grep"])
              open_stacks.pop().close()
              S.barrier()
              es = ExitStack()
              open_stacks.append(es)
              PTB = es.enter_context(nc.sbuf_tensor(f"PTB_{l}", [128, 4, 2, 256], BF16))
              OUN = es.enter_context(nc.sbuf_tensor(f"OUN_{l}", [128, NT, 128], F32))
              ONB = es.enter_context(nc.sbuf_tensor(f"ONB_{l}", [128, NT, 128], BF16))
              RS = es.enter_context(nc.sbuf_tensor(f"RS_{l}", [128, 2, 4], F32))
              JO = es.enter_context(nc.sbuf_tensor(f"JO_{l}", [128, 128], F32))
              chk("V" + str(l), VA=VA[:])
              W, wreg = next_block("GA")
              Wg = W[:, 0:4096].rearrange("p (k n) -> p k n", k=8)
              for c in range(4):
                  for tb in range(4):
                      def ev(b, c=c, tb=tb):
                          S.op("act", lambda: act.activation(out=YA[:, c, tb * 512:(tb + 1) * 512], in_=bank(b),
                                                             func=AF.Silu),
                               reads=[bankreg(b)], writes=[f"ya{c}_{tb}"])
                      proj_fm(Wg, wreg, c * 128, tb, ev)
              release_block()

              def qk_jobs(h, qbanks=(7,)):
                  jobs = []
                  holder = {}

                  def start():
                      W, wreg = next_block(f"QK{h}")
                      holder["W"] = W[:, 0:2048].rearrange("p (k n) -> p k n", k=8)
                      holder["wreg"] = wreg
                  for qk in range(2):
                      for tb in range(4):
                          def job(qk=qk, tb=tb, first=(qk == 0 and tb == 0), lastj=(qk == 1 and tb == 3)):
                              if first:
                                  start()

                              def ev(b):
                                  cs = slice(tb * 512, (tb + 1) * 512)
                                  if qk == 1:
                                      S.op("dve", lambda: dve.tensor_copy(out=QK[:, h % 2, 2, cs], in_=bank(b)),
                                           reads=[bankreg(b)], writes=[f"qk{h % 2}_1_{tb}"])
                                  else:
                                      S.op("dve", lambda: dve.tensor_copy(out=QK[0:64, h % 2, 0, cs],
                                                                          in_=PS[0:64, b, :]),
                                           reads=[bankreg(b), "qz0"], writes=[f"qk{h % 2}_0_{tb}"])
                                      S.op("dve", lambda: dve.tensor_copy(out=QK[64:128, h % 2, 1, cs],
                                                                          in_=PS[64:128, b, :]),
                                           reads=[bankreg(b), "qz1"], writes=[f"qk{h % 2}_0_{tb}"])
                              proj_fm(holder["W"], holder["wreg"], qk * 128, tb, ev, banks=qbanks)
                              if lastj:
                                  release_block()
                          jobs.append(job)
                  return jobs

              chk("GA" + str(l))
              for j in qk_jobs(0, (4, 5, 6, 7)):
                  j()
              chk("QK" + str(l), QK=QK[:], YA=YA[:])
              pend_te = []
              for h in range(4):
                  bg = qk_jobs(h + 1) if h < 3 else []
                  te_now = list(pend_te)
                  pend_te[:] = []
                  hb = h % 2
                  qorder = [7, 0, 6, 1, 5, 2, 4, 3]
                  items = [(pos, qb, kt) for pos, qb in enumerate(qorder) for kt in range(2 * qb + 2)]
                  n_items = len(items)
                  bg_every = max(1, n_items // (len(bg) + 1)) if bg else 0

                  def emit_S(i):
                      pos, qb, kt = items[i]
                      r = kt - 2 * qb
                      c0 = 128 if r == 1 else 0
                      sb = i % 3
                      Sb = PS[:, sb, :].rearrange("p (m n) -> p m n", m=2)

                      def fn():
                          for m in range(2):
                              ins = pe.matmul(out=Sb[:, m, c0:256],
                                              lhsT=QK[:, hb, 2, kt * 128:(kt + 1) * 128],
                                              rhs=QK[:, hb, m, qb * 256 + c0:(qb + 1) * 256],
                                              start=(m == 0), stop=(r < 0 and m == 1), skip_group_check=True)
                          if r >= 0:
                              ins = pe.matmul(out=PS[:, sb, r * 128:r * 128 + 320], lhsT=MA[0:1, :], rhs=MB[0:1, :],
                                              start=False, stop=True, skip_group_check=True)
                          return ins
                      S.op("pe", fn, reads=[f"qk{hb}_1_{kt // 4}", f"qk{hb}_0_{qb // 2}", "mska", "mskb"],
                           writes=[bankreg(sb)])

                  LA = 2
                  for i in range(min(LA, n_items)):
                      emit_S(i)
                  for i in range(n_items):
                      pos, qb, kt = items[i]
                      r = kt - 2 * qb
                      c0 = 128 if r == 1 else 0
                      sb = i % 3
                      pb_ = i % 4
                      Sb = PS[:, sb, :].rearrange("p (m n) -> p m n", m=2)
                      if i + LA < n_items:
                          emit_S(i + LA)
                      S.op("act", lambda: act.activation(out=PTB[:, pb_, :, c0:256], in_=Sb[:, :, c0:256], func=AF.Exp,
                                                         scale=0.125),
                           reads=[bankreg(sb)], writes=[f"pt{pb_}"])
                      ob = 3 + 2 * (pos % 2)
                      Ob = PS[:, ob:ob + 2, :].rearrange("p b (j c) -> p (b j) c", j=2)

                      def av():
                          ins = None
                          for m in range(2):
                              for j in range(2):
                                  if r == 1 and j == 0:
                                      continue
                                  a = m * 2 + j
                                  ins = pe.matmul(out=Ob[:, a, 0:130], lhsT=PTB[:, pb_, m, j * 128:(j + 1) * 128],
                                                  rhs=VA[:, kt, h, 0:130], start=(kt == 0 and j == 0),
                                                  stop=(kt == 2 * qb + j), skip_group_check=True)
                          return ins
                      S.op("pe", av, reads=[f"pt{pb_}", f"va{kt}", "va_ones"], writes=[bankreg(ob), bankreg(ob + 1)])
                      if kt == 2 * qb + 1:
                          rsb = RS[:, pos % 2, :]
                          S.op("dve", lambda: dve.reciprocal(out=rsb, in_=Ob[:, :, 128]),
                               reads=[bankreg(ob), bankreg(ob + 1)], writes=[f"rs{pos % 2}"])
                          S.op("dve", lambda: dve.tensor_scalar(out=rsb[:, 2:4], in0=rsb[:, 2:4], scalar1=NEGLAM,
                                                                scalar2=None, op0=ALU.mult),
                               reads=["neglam"], writes=[f"rs{pos % 2}"])
                          for j in range(2):
                              tt = 2 * qb + j
                              S.op("dve", lambda: dve.tensor_scalar(out=OUN[:, tt, :], in0=Ob[:, j, 0:128],
                                                                    scalar1=rsb[:, j:j + 1], scalar2=None, op0=ALU.mult),
                                   reads=[bankreg(ob), f"rs{pos % 2}"], writes=[f"oun{tt}"])
                              S.op("dve", lambda: dve.scalar_tensor_tensor(out=OUN[:, tt, :], in0=Ob[:, 2 + j, 0:128],
                                                                           scalar=rsb[:, 2 + j:3 + j], in1=OUN[:, tt, :],
                                                                           op0=ALU.mult, op1=ALU.add),
                                   reads=[bankreg(ob + 1), f"rs{pos % 2}"], writes=[f"oun{tt}"])
                              S.op("dve", lambda: dve.scalar_tensor_tensor(out=JO[:], in0=OUN[:, tt, :], scalar=1.0,
                                                                           in1=OUN[:, tt, :], op0=ALU.mult, op1=ALU.mult,
                                                                           accum_out=ST2[:, tt:tt + 1]),
                                   reads=[f"oun{tt}"], writes=["jo", f"oss{tt}"])
                      if bg and (i % bg_every == bg_every - 1):
                          bg.pop(0)()
                      if te_now and i in (3, 8, 13):
                          te_now.pop(0)()
                  while bg:
                      bg.pop(0)()
                  chk(f"H{h}_" + str(l))
                  def norm_job(h):
                      def job():
                          S.op("act", lambda: act.activation(out=ST2[:, 16:32], in_=ST2[:, 0:16], func=AF.Sqrt,
                                                             scale=1.0 / 128, bias=EPS),
                               reads=[f"oss{t}" for t in range(NT)], writes=["osq"])
                          S.op("dve", lambda: dve.reciprocal(out=ST2[:, 32:48], in_=ST2[:, 16:32]), reads=["osq"],
                               writes=["orstd"])
                          for t in range(NT):
                              S.op("dve", lambda t=t: dve.tensor_scalar(out=ONB[:, t, :], in0=OUN[:, t, :],
                                                                        scalar1=ST2[:, 32 + t:33 + t], scalar2=None,
                                                                        op0=ALU.mult),
                                   reads=[f"oun{t}", "orstd"], writes=[f"onb{t}"])
                      return job
                  def te_job(h, half):
                      def job():
                          pb = 7
                          PT_ = PS[:, pb, :].bitcast(BF16)

                          def tr():
                              for u in range(8):
                                  t = half * 8 + u
                                  ins = pe.transpose(out=PT_[:, u * 128:(u + 1) * 128], in_=ONB[:, t, :],
                                                     identity=IDENT[:])
                              return ins
                          S.op("pe", tr, reads=[f"onb{half * 8 + u}" for u in range(8)] + ["ident"],
                               writes=[bankreg(pb)])
                          S.op("dve", lambda: dve.scalar_tensor_tensor(
                              out=YA[:, h, half * 1024:(half + 1) * 1024], in0=PT_, scalar=SUBLN,
                              in1=YA[:, h, half * 1024:(half + 1) * 1024], op0=ALU.mult, op1=ALU.mult),
                              reads=[bankreg(pb), "subln_s"], writes=[f"ya{h}_{2 * half}", f"ya{h}_{2 * half + 1}"])
                      return job
                  pend_te[:] = [norm_job(h), te_job(h, 0), te_job(h, 1)]
                  if h == 3:
                      for j in pend_te:
                          j()
                      pend_te[:] = []
              open_stacks.pop().close()
          S.barrier()
          chk("B1" + str(l), YA=YA[:])

          with nc.sbuf_tensor(f"YB_{l}", [128, 4, T], BF16) as YB, nc.sbuf_tensor(f"YC_{l}", [128, 4, T], BF16) as YC:
              NP = 16 + 1040
              with nc.sbuf_tensor(f"PIN_{l}", [128, 2, NP], F32) as PIN, \
                      nc.sbuf_tensor(f"WA_{l}", [128, NP], F32) as WA, \
                      nc.sbuf_tensor(f"WB_{l}", [128, NP], F32) as WB, \
                      nc.sbuf_tensor(f"PL_{l}", [128, 2, 1024], BF16) as PL, \
                      nc.sbuf_tensor(f"T16_{l}", [128, 16], F32) as T16, \
                      nc.sbuf_tensor(f"TMPB_{l}", [128, 2, 512], BF16) as TMPB:
                  ALLB = (0, 1, 2, 3, 4, 5, 6, 7)
                  W, wregp = next_block("PIN")
                  Wp = W[:, 0:4096].rearrange("p (k n) -> p k n", k=8)
                  W, wregg = next_block("GB")
                  Wg = W[:, 0:4096].rearrange("p (k n) -> p k n", k=8)
                  S.op("dve", lambda: dve.memset(PIN[:, :, 0:16], 0.0), writes=["pin0", "pin1"])
                  S.op("dve", lambda: dve.memset(WA[:, 0:16], 0.0), writes=["wa"])
                  S.op("dve", lambda: dve.memset(WB[:, 0:16], 0.0), writes=["wb"])
                  units = [(g, hp) for g in (3, 2, 1, 0) for hp in range(2)]

                  def poolmm(g, hp):
                      for j in range(2):
                          tb = 2 * hp + j
                          b = (rr["b"]) % 8
                          rr["b"] += 1
                          S.op("pe", lambda: pe.matmul(out=bank(b), lhsT=PW[:, g, :], rhs=PL[:, hp, j * 512:(j + 1) * 512],
                                                       start=True, stop=True), reads=["pw", f"pl{hp}"], writes=[bankreg(b)])
                          tbf = tb % 2
                          S.op("act", lambda: act.activation(
                              out=TMPB[:, tbf, :], in_=bank(b), func=AF.Identity, scale=pp(l, 28 + g),
                              bias=ST[:, 10 + g:11 + g]), reads=[bankreg(b), "vpp", "bps"], writes=[f"tmpb{tbf}"])
                          S.op("pool", lambda: pool.tensor_tensor(
                              out=YB[:, g, tb * 512:(tb + 1) * 512], in0=YB[:, g, tb * 512:(tb + 1) * 512],
                              in1=TMPB[:, tbf, :], op=ALU.mult), reads=[f"tmpb{tbf}"], writes=[f"yb{g}_{tb}"])

                  prev = None
                  for ui, (g, hp) in enumerate(units):
                      h0 = 16 * hp
                      n = 1024 + h0
                      P_ = PIN[:, hp, :]
                      for j in range(2):
                          tb = 2 * hp + j

                          def ev(b, j=j):
                              c0 = 16 + h0 + j * 512
                              S.op("act", lambda: act.copy(out=P_[:, c0:c0 + 512], in_=bank(b)),
                                   reads=[bankreg(b)], writes=[f"pin{hp}"])
                              if hp == 0 and j == 1:
                                  S.op("act", lambda: act.copy(out=PIN[:, 1, 16:32], in_=PS[:, b, 496:512]),
                                       reads=[bankreg(b)], writes=["pin1"])
                          proj_fm(Wp, wregp, g * 128, tb, ev, banks=ALLB)
                      if ui == len(units) - 1:
                          release_block()
                      w = 2 ** (g + 1)
                      src, srcr = P_, f"pin{hp}"
                      bufs = [(WA, "wa"), (WB, "wb")]
                      sh = 1
                      bi = 0
                      while sh < w:
                          dst, dstr = bufs[bi % 2]
                          S.op("dve", lambda src=src, dst=dst, sh=sh: dve.tensor_tensor(
                              out=dst[:, 16:16 + n], in0=src[:, 16:16 + n], in1=src[:, 16 - sh:16 - sh + n], op=ALU.add),
                              reads=[srcr], writes=[dstr])
                          src, srcr = dst, dstr
                          sh *= 2
                          bi += 1
                      d0 = 16 + h0
                      S.op("dve", lambda src=src: dve.scalar_tensor_tensor(
                          out=PL[:, hp, :], in0=src[:, d0:d0 + 1024], scalar=1.0 / w, in1=P_[:, d0:d0 + 1024],
                          op0=ALU.mult, op1=ALU.subtract), reads=[srcr, f"pin{hp}"], writes=[f"pl{hp}"])
                      if hp == 0:
                          S.op("dve", lambda src=src: dve.tensor_tensor(out=T16[:], in0=src[:, 16:32], in1=RCNT[:, g, :],
                                                                        op=ALU.mult),
                               reads=[srcr, "rcnt"], writes=["t16"])
                          S.op("dve", lambda: dve.tensor_tensor(out=PL[:, 0, 0:16], in0=T16[:], in1=P_[:, 16:32],
                                                                op=ALU.subtract),
                               reads=["t16", "pin0"], writes=["pl0"])
                      for j in range(2):
                          tb = 2 * hp + j

                          def ev(b, tb=tb):
                              S.op("act", lambda: act.activation(out=YB[:, g, tb * 512:(tb + 1) * 512], in_=bank(b),
                                                                 func=AF.Silu),
                                   reads=[bankreg(b)], writes=[f"yb{g}_{tb}"])
                          proj_fm(Wg, wregg, g * 128, tb, ev, banks=ALLB)
                      if ui == len(units) - 1:
                          release_block()
                      if prev is not None:
                          poolmm(*prev)
                      prev = (g, hp)
                  poolmm(*prev)
              S.barrier()
              chk("B2" + str(l), YB=YB[:])
              with nc.sbuf_tensor(f"LNG_{l}", [128, 3, 512], F32) as LNG, \
                      nc.sbuf_tensor(f"GV_{l}", [128, NT, 512], BF16) as GV, \
                      nc.sbuf_tensor(f"TMPB2_{l}", [128, 2, 512], BF16) as TMPB, \
                      nc.sbuf_tensor(f"VN_{l}", [128, 1, 512], F32) as VN, \
                      nc.sbuf_tensor(f"VNB_{l}", [128, 2, 512], BF16) as VNB, \
                      nc.sbuf_tensor(f"BST_{l}", [128, NT, 6], F32) as BST, \
                      nc.sbuf_tensor(f"MV_{l}", [128, NT, 2], F32) as MV:
                  ALLB = (0, 1, 2, 3, 4, 5, 6, 7)
                  S.dma("sp", "lng0", LNG[:, 0, :], ln_g_d[:, l, :], writes=["lng0"])
                  S.dma("sp", "lng1", LNG[:, 1, :], ln_b_d[:, l, :], writes=["lng1"])
                  S.dma("sp", "lng2", LNG[:, 2, :], sgu_b_d[:, l, :], writes=["lng2"])
                  S.op("dve", lambda: dve.memset(SW[64:128, :, 0:64], 0.0), writes=["sw"])
                  W, wreg = next_block("SV")
                  Wv = W[:, 0:4096].rearrange("p (k n) -> p k n", k=8)
                  for t in range(NT):
                      def ev(b, t=t):
                          S.op("act", lambda: act.activation(out=GV[:, t, :], in_=bank(b), func=AF.Gelu),
                               reads=[bankreg(b)], writes=[f"gv{t}"])
                          S.op("dve", lambda: dve.bn_stats(out=BST[:, t, :], in_=GV[:, t, :]), reads=[f"gv{t}"],
                               writes=[f"bst{t}"])
                          S.op("dve", lambda: dve.bn_aggr(out=MV[:, t, :], in_=BST[:, t, :]), reads=[f"bst{t}"],
                               writes=[f"mv{t}"])
                      proj_tm(Wv, wreg, t, 512, ev, banks=ALLB)
                  release_block()
                  W, wregc = next_block("GC")
                  Wgc = W[:, 0:4096].rearrange("p (k n) -> p k n", k=8)
                  W, wregu = next_block("SU")
                  Wsu = W[:, 0:4096].rearrange("p (k n) -> p k n", k=8)

                  def gc_job(c, tb, rel):
                      def job():
                          def ev(b):
                              S.op("act", lambda: act.activation(out=YC[:, c, tb * 512:(tb + 1) * 512], in_=bank(b),
                                                                 func=AF.Silu),
                                   reads=[bankreg(b)], writes=[f"yc{c}_{tb}"])
                          proj_fm(Wgc, wregc, c * 128, tb, ev, banks=(4, 5, 6, 7))
                          if rel:
                              release_block()
                      return job

                  def su_job(c, tb, rel):
                      def job():
                          def ev(b):
                              tbf = rr["b"] % 2
                              S.op("act", lambda: act.activation(out=TMPB[:, tbf, :], in_=bank(b), func=AF.Gelu),
                                   reads=[bankreg(b)], writes=[f"tmpb{tbf}"])
                              S.op("pool", lambda: pool.tensor_tensor(
                                  out=YC[:, c, tb * 512:(tb + 1) * 512], in0=YC[:, c, tb * 512:(tb + 1) * 512],
                                  in1=TMPB[:, tbf, :], op=ALU.mult), reads=[f"tmpb{tbf}"], writes=[f"yc{c}_{tb}"])
                          proj_fm(Wsu, wregu, c * 128, tb, ev, banks=(4, 5, 6, 7))
                          if rel:
                              release_block()
                      return job
                  jobs = [gc_job(c, tb, rel=(tb == 3 and c == 3)) for tb in range(4) for c in range(4)]
                  jobs += [su_job(c, tb, rel=(tb == 3 and c == 3)) for tb in range(4) for c in range(4)]
                  for _ in range(8):
                      jobs.pop(0)()
                  S.op("act", lambda: act.activation(out=ST2[:, 32:48], in_=MV[:, :, 1], func=AF.Sqrt, scale=1.0,
                                                     bias=EPS), reads=[f"mv{t}" for t in range(NT)], writes=["ssq"])
                  S.op("dve", lambda: dve.reciprocal(out=ST2[:, 0:16], in_=ST2[:, 32:48]), reads=["ssq"],
                       writes=["srstd"])
                  S.op("dve", lambda: dve.scalar_tensor_tensor(out=ST2[:, 16:32], in0=MV[:, :, 0], scalar=-1.0,
                                                               in1=ST2[:, 0:16], op0=ALU.mult, op1=ALU.mult),
                       reads=["srstd"] + [f"mv{t}" for t in range(NT)], writes=["snmr"])
                  chk("SV" + str(l), GV=GV[:], MV=MV[:], ST2=ST2[:])

                  def norm(t):
                      vb = t % 2
                      S.op("act", lambda: act.activation(
                          out=VN[:, 0, :], in_=GV[:, t, :], func=AF.Identity, scale=ST2[:, t:t + 1],
                          bias=ST2[:, 16 + t:17 + t]), reads=[f"gv{t}", "srstd", "snmr"], writes=["vn0"])
                      S.op("dve", lambda: dve.tensor_tensor(out=VN[:, 0, :], in0=VN[:, 0, :], in1=LNG[:, 0, :],
                                                            op=ALU.mult), reads=["lng0"], writes=["vn0"])
                      S.op("dve", lambda: dve.tensor_tensor(out=VNB[:, vb, :], in0=VN[:, 0, :], in1=LNG[:, 1, :],
                                                            op=ALU.add), reads=["vn0", "lng1"],
                           writes=[f"vnb{vb}"])
                  norm(0)
                  for t in range(NT):
                      vb = t % 2
                      if t + 1 < NT:
                          norm(t + 1)
                      b = t % 4

                      def mix(vb=vb, b=b):
                          for g in range(4):
                              ins = pe.matmul(out=PS[:, b, g * 128:(g + 1) * 128], lhsT=VNB[:, vb, g * 128:(g + 1) * 128],
                                              rhs=SW[:, g, :], start=True, stop=True, skip_group_check=True)
                          return ins
                      S.op("pe", mix, reads=[f"vnb{vb}", "sw"], writes=[bankreg(b)])
                      S.op("dve", lambda b=b, t=t: dve.tensor_tensor(out=GV[:, t, :], in0=bank(b), in1=LNG[:, 2, :],
                                                                     op=ALU.add),
                           reads=[bankreg(b), "lng2"], writes=[f"gv{t}"])
                      S.op("dve", lambda t=t: dve.tensor_tensor(
                          out=YC[:, :, t * 128:(t + 1) * 128], in0=YC[:, :, t * 128:(t + 1) * 128],
                          in1=GV[:, t, :].rearrange("p (g i) -> p g i", g=4), op=ALU.mult),
                          reads=[f"gv{t}"], writes=[f"yc{c}_{t // 4}" for c in range(4)])
                      for _ in range(2 if t % 2 == 0 else 1):
                          if jobs:
                              jobs.pop(0)()
                  while jobs:
                      jobs.pop(0)()
              S.barrier()
              chk("B3" + str(l), YC=YC[:])
              with nc.sbuf_tensor(f"MG_{l}", [128, 8, 1024], BF16) as MG, \
                      nc.sbuf_tensor(f"GT_{l}", [128, 1, 3, 512], BF16) as GT, \
                      nc.sbuf_tensor(f"M0_{l}", [128, 1, 512], F32) as M0, \
                      nc.sbuf_tensor(f"M1_{l}", [128, 1, 512], F32) as M1, \
                      nc.sbuf_tensor(f"OT_{l}", [128, 2, 512], F32) as OT, \
                      nc.sbuf_tensor(f"JC_{l}", [128, 512], BF16) as JC:
                  Ys = [(YA, "ya"), (YB, "yb"), (YC, "yc")]
                  it = 0
                  for hf in range(2):
                      for c in range(8):
                          W, wreg = next_block(f"C{hf}_{c}")
                          for tbl in range(2):
                              tb = 2 * hf + tbl
                              gi = 0
                              it += 1
                              gb = []
                              for n in range(3):
                                  b = rr["b"] % 8
                                  rr["b"] += 1

                                  def fn(n=n, b=b, tb=tb, W=W):
                                      for k in range(8):
                                          ins = pe.matmul(out=bank(b), lhsT=W[:, (n * 8 + k) * 128:(n * 8 + k + 1) * 128],
                                                          rhs=HTb(k, tb), start=(k == 0), stop=(k == 7))
                                      return ins
                                  S.op("pe", fn, reads=[wreg, f"ht{tb}"], writes=[bankreg(b)])
                                  S.op("act", lambda n=n, b=b, gi=gi: act.activation(
                                      out=GT[:, gi, n, :], in_=bank(b), func=AF.Sigmoid, bias=pp(l, n * 8 + c)),
                                      reads=[bankreg(b), "vpp"], writes=[f"gt{gi}_{n}"])
                              bb = []
                              for n in range(3):
                                  b = rr["b"] % 8
                                  rr["b"] += 1
                                  bb.append(b)
                                  Yn, yname = Ys[n]

                                  def fn(n=n, b=b, tb=tb, W=W, Yn=Yn):
                                      for k in range(4):
                                          o = 3072 + (n * 4 + k) * 128
                                          ins = pe.matmul(out=bank(b), lhsT=W[:, o:o + 128],
                                                          rhs=Yn[:, k, tb * 512:(tb + 1) * 512],
                                                          start=(k == 0), stop=(k == 3))
                                      return ins
                                  S.op("pe", fn, reads=[wreg] + [f"{yname}{k}_{tb}" for k in range(4)],
                                       writes=[bankreg(b)])
                              S.op("dve", lambda gi=gi, b=bb[0]: dve.tensor_tensor(out=M0[:, gi, :], in0=bank(b),
                                                                                   in1=GT[:, gi, 0, :], op=ALU.mult),
                                   reads=[bankreg(bb[0]), f"gt{gi}_0"], writes=[f"m0_{gi}"])
                              S.op("dve", lambda gi=gi, b=bb[1]: dve.tensor_tensor(out=M1[:, gi, :], in0=bank(b),
                                                                                   in1=GT[:, gi, 1, :], op=ALU.mult),
                                   reads=[bankreg(bb[1]), f"gt{gi}_1"], writes=[f"m1_{gi}"])
                              S.op("pool", lambda gi=gi: pool.tensor_tensor(out=M0[:, gi, :], in0=M0[:, gi, :],
                                                                            in1=M1[:, gi, :], op=ALU.add),
                                   reads=[f"m1_{gi}"], writes=[f"m0_{gi}"])
                              S.op("dve", lambda gi=gi, b=bb[2]: dve.tensor_tensor(out=M1[:, gi, :], in0=bank(b),
                                                                                   in1=GT[:, gi, 2, :], op=ALU.mult),
                                   reads=[bankreg(bb[2]), f"gt{gi}_2"], writes=[f"m1_{gi}"])
                              S.op("pool", lambda gi=gi, c=c, tbl=tbl: pool.tensor_tensor(
                                  out=MG[:, c, tbl * 512:(tbl + 1) * 512], in0=M0[:, gi, :], in1=M1[:, gi, :],
                                  op=ALU.add), reads=[f"m0_{gi}", f"m1_{gi}"], writes=[f"mg{c}_{tbl}"])
                          release_block()
                      W0, wreg0 = next_block(f"WO{hf}_0")
                      W1, wreg1 = next_block(f"WO{hf}_1")
                      Wo = [W0[:, 0:4096].rearrange("p (k n) -> p k n", k=8),
                            W1[:, 0:4096].rearrange("p (k n) -> p k n", k=8)]
                      wor = [wreg0, wreg1]
                      JX = M1[:, 0, :].bitcast(BF16)

                      def ssq_next(tt):
                          S.op("act", lambda: act.activation(out=JX, in_=X[:, tt, :], func=AF.Square,
                                                             accum_out=ST[:, 48 + tt:49 + tt]),
                               reads=[f"x{tt}"], writes=["m1_0", f"ssn{tt}"])
                      for tl in range(8):
                          t = hf * 8 + tl
                          obs = []
                          for hc in range(2):
                              b = rr["b"] % 8
                              rr["b"] += 1
                              obs.append(b)

                              def fn(hc=hc, b=b, tl=tl):
                                  for k in range(8):
                                      ins = pe.matmul(out=bank(b), lhsT=MG[:, k, tl * 128:(tl + 1) * 128],
                                                      rhs=Wo[hc][:, k, :], start=(k == 0), stop=(k == 7))
                                  return ins
                              S.op("pe", fn, reads=[wor[hc]] + [f"mg{k}_{tl // 4}" for k in range(8)],
                                   writes=[bankreg(b)])
                              S.op("act", lambda hc=hc, b=b, t=t: act.activation(
                                  out=JC[:], in_=bank(b), func=AF.Square, accum_out=ST[:, 16 + 2 * t + hc:17 + 2 * t + hc]),
                                  reads=[bankreg(b)], writes=["jc", f"pss{t}_{hc}"])
                          S.op("dve", lambda t=t: dve.tensor_tensor(out=ST2[:, t:t + 1], in0=ST[:, 16 + 2 * t:17 + 2 * t],
                                                                    in1=ST[:, 17 + 2 * t:18 + 2 * t], op=ALU.add),
                               reads=[f"pss{t}_0", f"pss{t}_1"], writes=[f"pssum{t}"])
                          S.op("act", lambda t=t: act.activation(out=ST2[:, 16 + t:17 + t], in_=ST2[:, t:t + 1],
                                                                 func=AF.Sqrt, scale=1.0 / D, bias=EPS),
                               reads=[f"pssum{t}"], writes=[f"psq{t}"])
                          S.op("dve", lambda t=t: dve.reciprocal(out=ST2[:, 32 + t:33 + t], in_=ST2[:, 16 + t:17 + t]),
                               reads=[f"psq{t}"], writes=[f"prstd{t}"])
                          for hc in range(2):
                              b = obs[hc]
                              ob_ = (2 * tl + hc) % 2
                              S.op("dve", lambda hc=hc, b=b, t=t, ob_=ob_: dve.scalar_tensor_tensor(
                                  out=OT[:, ob_, :], in0=bank(b), scalar=ST2[:, 32 + t:33 + t],
                                  in1=GREP[:, hc * 512:(hc + 1) * 512], op0=ALU.mult, op1=ALU.mult),
                                  reads=[bankreg(b), f"prstd{t}", "grep"], writes=[f"ot{ob_}"])
                              eng_ = "pool" if hc == 0 else "dve"
                              E_ = pool if hc == 0 else dve
                              S.op(eng_, lambda hc=hc, t=t, ob_=ob_, E_=E_: E_.tensor_tensor(
                                  out=X[:, t, hc * 512:(hc + 1) * 512], in0=X[:, t, hc * 512:(hc + 1) * 512],
                                  in1=OT[:, ob_, :], op=ALU.add), reads=[f"ot{ob_}"], writes=[f"x{t}"])
                          if last:
                              S.dma("sp", "out", y_d[t * 128:(t + 1) * 128, :], X[:, t, :], reads=[f"x{t}"])
                          elif tl >= 1:
                              ssq_next(t - 1)
                      if not last:
                          ssq_next(hf * 8 + 7)
                      release_block()
                      release_block()
              S.barrier()
              chk("C" + str(l))
    except _Stop:
        S.barrier(hard=True)
        for t in range(NT):
            S.dma("sp", "out", y_d[t * 128:(t + 1) * 128, :], X[:, t, :], reads=[f"x{t}"])
    nc.sync.wait_ge(S.dsem["out"], S.dcnt["out"])
    return nc


def _prep_weights(inp):
    f = np.float32
    w_in = np.asarray(inp["w_in"], f)
    cols = list(range(1024, 1536)) + list(range(1536, 2048))
    for h in range(4):
        cols += list(range(h * 128, (h + 1) * 128)) + list(range(512 + h * 128, 512 + (h + 1) * 128))
    cols += list(range(2560, 3072)) + list(range(2048, 2560)) + list(range(4096, 4608)) + \
        list(range(3072, 3584)) + list(range(3584, 4096))
    cols = np.array(cols)
    w_in_p = np.ascontiguousarray(w_in[:, :, cols].reshape(L, 8, 128, 4608).transpose(0, 2, 1, 3))
    w_merge = np.asarray(inp["w_merge"], f)
    w_branch = np.asarray(inp["w_branch"], f)
    wm = w_merge.reshape(L, 8, 128, 3, 8, 128)
    wm = wm.transpose(0, 4, 2, 3, 1, 5).reshape(L, 8, 128, 3072)
    wb = w_branch.reshape(L, 3, 4, 128, 8, 128)
    wb = wb.transpose(0, 4, 3, 1, 2, 5).reshape(L, 8, 128, 1536)
    w_c = np.ascontiguousarray(np.concatenate([wm, wb], axis=-1))
    w_out = np.asarray(inp["w_out"], f)
    w_out_p = np.ascontiguousarray(w_out.reshape(L, 8, 128, 1024).transpose(0, 2, 1, 3))
    pool_w_p = np.ascontiguousarray(np.asarray(inp["pool_w"], f).transpose(0, 2, 1, 3))
    sgu_wT = np.ascontiguousarray(np.asarray(inp["sgu_w"], f).transpose(0, 3, 1, 2))
    vec = np.zeros((128, L * NPP), f)
    for l in range(L):
        vec[:, l * NPP:l * NPP + 24] = np.asarray(inp["b_merge"], f)[l].reshape(24, 128).T
        vec[:, l * NPP + 24:l * NPP + 28] = np.asarray(inp["pool_b"], f)[l].T
        vec[:, l * NPP + 28:l * NPP + 32] = np.asarray(inp["pool_scale"], f)[l].reshape(4, 128).T
        vec[:, l * NPP + 32] = np.asarray(inp["attn_subln_g"], f)[l]

    def rep(a):
        a = np.asarray(a, f)
        return np.ascontiguousarray(np.broadcast_to(a[None], (128,) + a.shape))
    lam = np.stack([np.asarray(inp[k], f) for k in ("lambda_q1", "lambda_k1", "lambda_q2", "lambda_k2")], axis=1)
    rc = np.zeros((4, 16), f)
    for g in range(4):
        w = 2 ** (g + 1)
        for t in range(16):
            rc[g, t] = 1.0 / min(t + 1, w)
    return {
        "w_in_p": w_in_p, "w_c": w_c, "w_out_p": w_out_p, "pool_w_p": pool_w_p, "sgu_wT": sgu_wT,
        "vec_pp": vec, "pre_g_rep": rep(inp["pre_norm_g"]), "post_g_rep": rep(inp["post_norm_g"]),
        "ln_g_rep": rep(inp["sgu_ln_g"]), "ln_b_rep": rep(inp["sgu_ln_b"]),
        "sgu_b_rep": rep(np.asarray(inp["sgu_b"], f).reshape(L, 512)), "lam_rep": rep(lam),
        "ident": np.eye(128, dtype=f), "rcnt": rep(rc),
        "mask_a": (np.arange(128) >= 64).astype(f)[None, :], "mask_b": np.concatenate([np.full((1, 64), -30000.0, f), np.zeros((1, 192), f), np.full((1, 64), -30000.0, f)], axis=1),
    }


def kernel(**inputs):
    x = np.asarray(inputs["x"], np.float32)
    B = x.shape[0]
    shared = _prep_weights(inputs)
    nc = build()
    in_maps = []
    for b in range(B):
        m = dict(shared)
        m["x"] = np.ascontiguousarray(x[b])
        in_maps.append(m)
    res = run_bass_kernel_spmd(nc, in_maps, core_ids=list(range(B)))
    return np.stack([np.asarray(r["y"], np.float32) for r in res.results], axis=0)
```

```python
import math
from contextlib import ExitStack
import numpy as np
import concourse.bass as bass
import concourse.mybir as mybir
from concourse.bass_utils import run_bass_kernel_spmd

F32 = mybir.dt.float32
BF16 = mybir.dt.bfloat16
AF = mybir.ActivationFunctionType
ALU = mybir.AluOpType

L = 2
T = 2048
D = 1024
NT = 16
EPS = 1e-6
NS = 3
SLOT = 4608
NPP = 33


class Region:
    __slots__ = ("w", "rs")

    def __init__(self):
        self.w = None
        self.rs = {}


class Sched:
    def __init__(self, nc):
        self.nc = nc
        self.E = {"pe": nc.tensor, "act": nc.scalar, "dve": nc.vector, "pool": nc.gpsimd, "sp": nc.sync}
        self.sem = {k: nc.alloc_semaphore("s_" + k) for k in self.E}
        self.cnt = {k: 0 for k in self.E}
        self.seen = {k: {} for k in self.E}
        self.regs = {}
        self.dsem = {}
        self.dcnt = {}
        self.pending = {k: {} for k in self.E}

    def reg(self, name):
        r = self.regs.get(name)
        if r is None:
            r = self.regs[name] = Region()
        return r

    def _R(self, xs):
        return [self.reg(x) if isinstance(x, str) else x for x in xs]

    def _deps(self, reads, writes):
        deps = {}
        for r in reads:
            if r.w is not None:
                k, v = r.w
                deps[k] = max(deps.get(k, 0), v)
        for w in writes:
            if w.w is not None:
                k, v = w.w
                deps[k] = max(deps.get(k, 0), v)
            for k, v in w.rs.items():
                deps[k] = max(deps.get(k, 0), v)
        return deps

    def _wait(self, eng, deps):
        for k, v in deps.items():
            if k == "pe" and eng == "pe":
                continue
            if self.seen[eng].get(k, 0) >= v:
                continue
            s = self.sem[k] if k in self.sem else self.dsem[k]
            self.E[eng].wait_ge(s, v)
            self.seen[eng][k] = v

    def _commit(self, t, reads, writes):
        k, v = t
        for r in reads:
            r.rs[k] = max(r.rs.get(k, 0), v)
        for w in writes:
            w.w = t
            w.rs = {}

    def _apply_pending(self, eng):
        p = self.pending[eng]
        if p:
            self.pending[eng] = {}
            self._wait(eng, p)

    def op(self, eng, fn, reads=(), writes=()):
        reads = self._R(reads)
        writes = self._R(writes)
        if eng != "pe":
            self._apply_pending(eng)
        self._wait(eng, self._deps(reads, writes))
        ins = fn()
        self.cnt[eng] += 1
        ins.then_inc(self.sem[eng], 1)
        t = (eng, self.cnt[eng])
        self._commit(t, reads, writes)
        return t

    def dma(self, q, semname, out, in_, reads=(), writes=()):
        reads = self._R(reads)
        writes = self._R(writes)
        if semname not in self.dsem:
            self.dsem[semname] = self.nc.alloc_semaphore("d_" + semname)
            self.dcnt[semname] = 0
        self._apply_pending(q)
        self._wait(q, self._deps(reads, writes))
        self.E[q].dma_start(out=out, in_=in_).then_inc(self.dsem[semname], 16)
        self.dcnt[semname] += 16
        t = (semname, self.dcnt[semname])
        self._commit(t, reads, writes)
        return t

    def barrier(self, hard=False):
        deps = {k: self.cnt[k] for k in ("pe", "act", "dve", "pool") if self.cnt[k] > 0}
        if hard:
            for k, v in self.dcnt.items():
                deps[k] = v
        for e in ("act", "dve", "pool", "sp"):
            p = self.pending[e]
            for k, v in deps.items():
                p[k] = max(p.get(k, 0), v)
        if hard:
            self._wait("pe", deps)
            for e in ("act", "dve", "pool", "sp"):
                self._apply_pending(e)


class _Stop(Exception):
    pass


def build(n_layers=L, stop=None):
    nc = bass.Bass("TRN2", target_bir_lowering=False)
    S = Sched(nc)
    reg = S.reg
    pe, act, dve, pool = nc.tensor, nc.scalar, nc.vector, nc.gpsimd

    def din(name, shape):
        return nc.dram_tensor(name, shape, F32, kind="ExternalInput").ap()

    x_d = din("x", [T, D])
    w_in_d = din("w_in_p", [L, 128, 8, 4608])
    w_c_d = din("w_c", [L, 8, 128, SLOT])
    w_out_d = din("w_out_p", [L, 128, 8, 1024])
    pool_w_d = din("pool_w_p", [L, 128, 4, 128])
    sgu_w_d = din("sgu_wT", [L, 128, 4, 128])
    vec_pp_d = din("vec_pp", [128, L * NPP])
    pre_g_d = din("pre_g_rep", [128, L, 1024])
    post_g_d = din("post_g_rep", [128, L, 1024])
    ln_g_d = din("ln_g_rep", [128, L, 512])
    ln_b_d = din("ln_b_rep", [128, L, 512])
    sgu_b_d = din("sgu_b_rep", [128, L, 512])
    lam_d = din("lam_rep", [128, L, 4, 64])
    ident_d = din("ident", [128, 128])
    rcnt_d = din("rcnt", [128, 4, 16])
    mask_a_d = din("mask_a", [1, 128])
    mask_b_d = din("mask_b", [1, 320])
    y_d = nc.dram_tensor("y", [T, D], F32, kind="ExternalOutput").ap()

    X = nc.alloc_sbuf_tensor("X", [128, NT, D], F32)
    HT = nc.alloc_sbuf_tensor("HT", [128, 8, T], BF16)
    YA = nc.alloc_sbuf_tensor("YA", [128, 4, T], BF16)
    RING = nc.alloc_sbuf_tensor("RING", [128, NS, SLOT], BF16)
    GREP = nc.alloc_sbuf_tensor("GREP", [128, D], F32)
    IDENT = nc.alloc_sbuf_tensor("IDENT", [128, 128], BF16)
    VPP = nc.alloc_sbuf_tensor("VPP", [128, L * NPP], F32)
    PW = nc.alloc_sbuf_tensor("PW", [128, 4, 128], BF16)
    SW = nc.alloc_sbuf_tensor("SW", [128, 4, 128], BF16)
    RCNT = nc.alloc_sbuf_tensor("RCNT", [128, 4, 16], F32)
    ST = nc.alloc_sbuf_tensor("ST", [128, 64], F32)
    ST2 = nc.alloc_sbuf_tensor("ST2", [128, 64], F32)
    MA = nc.alloc_sbuf_tensor("MA", [1, 128], BF16)
    MB = nc.alloc_sbuf_tensor("MB", [1, 320], BF16)
    PS = nc.alloc_psum_tensor("PS", [128, 8, 512], F32)

    def bank(b):
        return PS[:, b, :]

    def bankreg(b):
        return reg(f"ps{b}")

    blocks = []
    for l in range(n_layers):
        for kind in ["V", "GA", "QK0", "QK1", "QK2", "QK3", "PIN", "GB", "SV", "GC", "SU"]:
            blocks.append((l, kind))
        for hf in range(2):
            for c in range(8):
                blocks.append((l, f"C{hf}_{c}"))
            blocks.append((l, f"WO{hf}_0"))
            blocks.append((l, f"WO{hf}_1"))
    WIN_OFF = {"V": (0, 512), "GA": (512, 512), "QK0": (1024, 256), "QK1": (1280, 256), "QK2": (1536, 256),
               "QK3": (1792, 256), "GB": (2048, 512), "PIN": (2560, 512), "GC": (3072, 512), "SU": (3584, 512),
               "SV": (4096, 512)}
    state = {"next_load": 0, "next_use": 0, "released": 0}

    def issue_load(extra_reads=()):
        i = state["next_load"]
        if i >= len(blocks):
            return
        state["next_load"] += 1
        l, kind = blocks[i]
        s = i % NS
        slot = RING[:, s, :]
        if kind in WIN_OFF:
            c0, n = WIN_OFF[kind]
            out = slot[:, 0:8 * n].rearrange("p (k n) -> p k n", k=8)
            src = w_in_d[l, :, :, c0:c0 + n]
        elif kind.startswith("C"):
            c = int(kind.split("_")[1])
            out = slot
            src = w_c_d[l, c]
        else:
            hc = int(kind.split("_")[1])
            out = slot[:, 0:4096].rearrange("p (k n) -> p k n", k=8)
            src = w_out_d[l, :, :, hc * 512:(hc + 1) * 512]
        S.dma("pool", f"ring{s}", out, src, reads=list(extra_reads), writes=[f"ring{s}"])

    def next_block(expect):
        i = state["next_use"]
        state["next_use"] += 1
        assert blocks[i][1] == expect, (blocks[i], expect)
        assert state["next_load"] > i
        s = i % NS
        return RING[:, s, :], f"ring{s}"

    def release_block():
        state["released"] += 1
        while state["next_load"] < min(len(blocks), state["released"] + NS):
            issue_load()

    S.dma("sp", "vpp", VPP[:], vec_pp_d, writes=["vpp"])
    S.dma("sp", "rcnt", RCNT[:], rcnt_d, writes=["rcnt"])
    for t in range(NT):
        S.dma("sp", f"x{t // 4}", X[:, t, :], x_d[t * 128:(t + 1) * 128, :], writes=[f"x{t}"])
    for t in range(NT):
        reg(f"x{t}").w = (f"x{t // 4}", 64)
    S.dma("pool", "ident", IDENT[:], ident_d, writes=["ident"])
    S.dma("pool", "mska", MA[:], mask_a_d, writes=["mska"])
    S.dma("pool", "mskb", MB[:], mask_b_d, writes=["mskb"])
    issue_load(extra_reads=["x0"])
    for _ in range(NS - 1):
        issue_load(extra_reads=[f"x{t}" for t in range(NT)])

    def pp(l, j):
        return VPP[:, l * NPP + j: l * NPP + j + 1]

    open_stacks = []

    def chk(tag, **tens):
        if stop == tag:
            S.barrier(hard=True)
            for name, ap in tens.items():
                d = nc.dram_tensor("dbg_" + name, list(ap.shape), F32, kind="ExternalOutput").ap()
                S.dma("pool", "dbg", d, ap)
            nc.gpsimd.wait_ge(S.dsem["dbg"], S.dcnt["dbg"])
            while open_stacks:
                open_stacks.pop().close()
            raise _Stop()

    try:
      for l in range(n_layers):
          lam_init = 0.8 - 0.6 * math.exp(-0.3 * l)
          last = (l == n_layers - 1)
          S.dma("act" if l == 0 else "sp", "grep", GREP[:], pre_g_d[:, l, :], writes=["grep"])
          S.dma("pool", "pw", PW[:], pool_w_d[l], writes=["pw"])
          S.dma("pool", "sw", SW[:], sgu_w_d[l], writes=["sw"])
          S.op("dve", lambda: dve.tensor_scalar(out=ST[:, 9:10], in0=pp(l, 32), scalar1=(1.0 - lam_init),
                                                scalar2=None, op0=ALU.mult),
               reads=["vpp"], writes=["subln_s"])
          S.op("dve", lambda: dve.tensor_tensor(out=ST[:, 10:14], in0=VPP[:, l * NPP + 24:l * NPP + 28],
                                                in1=VPP[:, l * NPP + 28:l * NPP + 32], op=ALU.mult),
               reads=["vpp"], writes=["bps"])
          NEGLAM = ST[:, 8:9]
          SUBLN = ST[:, 9:10]

          def HTb(k, tb):
              return HT[:, k, tb * 512:(tb + 1) * 512]

          rr = {"b": 0}

          def proj_fm(W, wreg, col0, tb, evac, banks=(4, 5, 6, 7)):
              b = banks[rr["b"] % len(banks)]
              rr["b"] += 1

              def fn():
                  for k in range(8):
                      ins = pe.matmul(out=bank(b), lhsT=W[:, k, col0:col0 + 128], rhs=HTb(k, tb),
                                      start=(k == 0), stop=(k == 7))
                  return ins
              S.op("pe", fn, reads=[wreg, f"ht{tb}"], writes=[bankreg(b)])
              evac(b)

          def proj_tm(W, wreg, t, n, evac, banks=(4, 5, 6, 7)):
              b = banks[rr["b"] % len(banks)]
              rr["b"] += 1

              def fn():
                  for k in range(8):
                      ins = pe.matmul(out=PS[:, b, 0:n], lhsT=HT[:, k, t * 128:(t + 1) * 128], rhs=W[:, k, 0:n],
                                      start=(k == 0), stop=(k == 7))
                  return ins
              S.op("pe", fn, reads=[wreg, f"ht{t // 4}"], writes=[bankreg(b)])
              evac(b)

          with nc.sbuf_tensor(f"LAMV_{l}", [128, 4, 64], F32) as LAMV, \
                  nc.sbuf_tensor(f"VA_{l}", [128, NT, 4, 130], BF16) as VA, \
                  nc.sbuf_tensor(f"QK_{l}", [128, 2, 3, T], BF16) as QK:
              es = ExitStack()
              open_stacks.append(es)
              XS = es.enter_context(nc.sbuf_tensor(f"XS_{l}", [128, 2, D], BF16))
              JA = es.enter_context(nc.sbuf_tensor(f"JA_{l}", [128, D], BF16))
              S.dma("act" if l == 0 else "sp", "lamv", LAMV[:], lam_d[:, l], writes=["lamv"])
              S.op("dve", lambda: dve.scalar_tensor_tensor(out=LAMV[:, 0, :], in0=LAMV[:, 0, :], scalar=1.0, in1=LAMV[:, 1, :],
                                                            op0=ALU.mult, op1=ALU.mult, accum_out=ST[:, 0:1]),
                   reads=["lamv"], writes=["lamv", "st_lam0"])
              S.op("dve", lambda: dve.scalar_tensor_tensor(out=LAMV[:, 2, :], in0=LAMV[:, 2, :], scalar=1.0, in1=LAMV[:, 3, :],
                                                            op0=ALU.mult, op1=ALU.mult, accum_out=ST[:, 1:2]),
                   reads=["lamv"], writes=["lamv", "st_lam1"])
              S.op("act", lambda: act.activation(out=ST[:, 2:4], in_=ST[:, 0:2], func=AF.Exp),
                   reads=["st_lam0", "st_lam1"], writes=["st_lam2"])
              S.op("dve", lambda: dve.tensor_tensor(out=ST[:, 4:5], in0=ST[:, 3:4], in1=ST[:, 2:3], op=ALU.subtract),
                   reads=["st_lam2"], writes=["st_lam3"])
              S.op("dve", lambda: dve.tensor_scalar(out=ST[:, 8:9], in0=ST[:, 4:5], scalar1=-lam_init, scalar2=None,
                                                    op0=ALU.add),
                   reads=["st_lam3"], writes=["neglam"])
              S.op("dve", lambda: dve.memset(QK[64:128, :, 0, :], 0.0), writes=["qz0"])
              S.op("dve", lambda: dve.memset(QK[0:64, :, 1, :], 0.0), writes=["qz1"])
              W, wreg = next_block("V")
              Wv = W[:, 0:4096].rearrange("p (k n) -> p k n", k=8)
              S.op("dve", lambda: dve.memset(VA[:, :, :, 128:130], 1.0), writes=["va_ones"])

              def a_stats(g):
                  if l == 0:
                      for t in range(4 * g, 4 * g + 4):
                          S.op("act", lambda t=t: act.activation(out=JA[:], in_=X[:, t, :], func=AF.Square,
                                                                  accum_out=ST2[:, 16 + t:17 + t]),
                               reads=[f"x{t}"], writes=["ja", f"ss{t}"])
                      S.op("act", lambda: act.activation(out=ST2[:, 32 + 4 * g:36 + 4 * g],
                                                         in_=ST2[:, 16 + 4 * g:20 + 4 * g],
                                                         func=AF.Sqrt, scale=1.0 / D, bias=EPS),
                           reads=[f"ss{t}" for t in range(4 * g, 4 * g + 4)], writes=[f"sq{g}"])
                  else:
                      S.op("act", lambda: act.activation(out=ST2[:, 32 + 4 * g:36 + 4 * g],
                                                         in_=ST[:, 48 + 4 * g:52 + 4 * g],
                                                         func=AF.Sqrt, scale=1.0 / D, bias=EPS),
                           reads=[f"ssn{t}" for t in range(4 * g, 4 * g + 4)], writes=[f"sq{g}"])

              def a_recip(g):
                  S.op("dve", lambda: dve.reciprocal(out=ST2[:, 48 + 4 * g:52 + 4 * g], in_=ST2[:, 32 + 4 * g:36 + 4 * g]),
                       reads=[f"sq{g}"], writes=[f"rstd{g}"])

              def a_tile(t):
                  b = t % 2
                  S.op("dve", lambda: dve.scalar_tensor_tensor(
                      out=XS[:, b, :], in0=X[:, t, :], scalar=ST2[:, 48 + t:49 + t], in1=GREP[:],
                      op0=ALU.mult, op1=ALU.mult), reads=[f"x{t}", f"rstd{t // 4}", "grep"], writes=[f"xs{b}"])
                  pb = 6 + b
                  PT_ = PS[:, pb, :].bitcast(BF16).rearrange("p (k n) -> p k n", k=8)

                  def tr():
                      for k in range(8):
                          ins = pe.transpose(out=PT_[:, k, :], in_=XS[:, b, k * 128:(k + 1) * 128], identity=IDENT[:])
                      return ins
                  S.op("pe", tr, reads=[f"xs{b}", "ident"], writes=[bankreg(pb)])
                  S.op("act", lambda: act.copy(out=HT[:, :, t * 128:(t + 1) * 128], in_=PT_),
                       reads=[bankreg(pb)], writes=[f"ht{t // 4}"])

              def v_tile(t):
                  def ev(b):
                      if l == 0:
                          S.op("dve", lambda: dve.tensor_copy(out=VA[:, t, :, 0:128],
                                                              in_=PS[:, b, :].rearrange("p (h d) -> p h d", h=4)),
                               reads=[bankreg(b)], writes=[f"va{t}"])
                      else:
                          S.op("act", lambda: act.copy(out=VA[:, t, :, 0:128],
                                                       in_=PS[:, b, :].rearrange("p (h d) -> p h d", h=4)),
                               reads=[bankreg(b)], writes=[f"va{t}"])
                  proj_tm(Wv, wreg, t, 512, ev, banks=(2, 3, 4, 5))
              if l == 0:
                  a_stats(0)
                  a_recip(0)
              else:
                  S.op("act", lambda: act.activation(out=ST2[:, 32:48], in_=ST[:, 48:64], func=AF.Sqrt,
                                                     scale=1.0 / D, bias=EPS),
                       reads=[f"ssn{t}" for t in range(NT)], writes=[f"sq{g}" for g in range(4)])
                  S.op("dve", lambda: dve.reciprocal(out=ST2[:, 48:64], in_=ST2[:, 32:48]),
                       reads=[f"sq{g}" for g in range(4)], writes=[f"rstd{g}" for g in range(4)])
              for g in range(4):
                  if l == 0 and g + 1 < 4:
                      a_stats(g + 1)
                  for t in range(4 * g, 4 * g + 4):
                      a_tile(t)
                  if l == 0 and g + 1 < 4:
                      a_recip(g + 1)
                  if g >= 1:
                      for t in range(4 * g - 4, 4 * g):
                          v_tile(t)
              for t in range(12, 16):
                  v_tile(t)
              release_block()
              chk("A" + str(l), HT=HT[:])
              S.dma("sp", "grep", GREP[:], post_g_d[:, l, :], writes=["grep"])
              open_stacks.pop().close()
              S.barrier()
              es = ExitStack()
              open_stacks.append(es)
              PTB = es.enter_context(nc.sbuf_tensor(f"PTB_{l}", [128, 4, 2, 256], BF16))
              OUN = es.enter_context(nc.sbuf_tensor(f"OUN_{l}", [128, NT, 128], F32))
              ONB = es.enter_context(nc.sbuf_tensor(f"ONB_{l}", [128, NT, 128], BF16))
              RS = es.enter_context(nc.sbuf_tensor(f"RS_{l}", [128, 2, 4], F32))
              JO = es.enter_context(nc.sbuf_tensor(f"JO_{l}", [128, 128], F32))
              chk("V" + str(l), VA=VA[:])
              W, wreg = next_block("GA")
              Wg = W[:, 0:4096].rearrange("p (k n) -> p k n", k=8)
              for c in range(4):
                  for tb in range(4):
                      def ev(b, c=c, tb=tb):
                          S.op("act", lambda: act.activation(out=YA[:, c, tb * 512:(tb + 1) * 512], in_=bank(b),
                                                             func=AF.Silu),
                               reads=[bankreg(b)], writes=[f"ya{c}_{tb}"])
                      proj_fm(Wg, wreg, c * 128, tb, ev)
              release_block()

              def qk_jobs(h, qbanks=(7,)):
                  jobs = []
                  holder = {}

                  def start():
                      W, wreg = next_block(f"QK{h}")
                      holder["W"] = W[:, 0:2048].rearrange("p (k n) -> p k n", k=8)
                      holder["wreg"] = wreg
                  for qk in range(2):
                      for tb in range(4):
                          def job(qk=qk, tb=tb, first=(qk == 0 and tb == 0), lastj=(qk == 1 and tb == 3)):
                              if first:
                                  start()

                              def ev(b):
                                  cs = slice(tb * 512, (tb + 1) * 512)
                                  if qk == 1:
                                      S.op("dve", lambda: dve.tensor_copy(out=QK[:, h % 2, 2, cs], in_=bank(b)),
                                           reads=[bankreg(b)], writes=[f"qk{h % 2}_1_{tb}"])
                                  else:
                                      S.op("dve", lambda: dve.tensor_copy(out=QK[0:64, h % 2, 0, cs],
                                                                          in_=PS[0:64, b, :]),
                                           reads=[bankreg(b), "qz0"], writes=[f"qk{h % 2}_0_{tb}"])
                                      S.op("dve", lambda: dve.tensor_copy(out=QK[64:128, h % 2, 1, cs],
                                                                          in_=PS[64:128, b, :]),
                                           reads=[bankreg(b), "qz1"], writes=[f"qk{h % 2}_0_{tb}"])
                              proj_fm(holder["W"], holder["wreg"], qk * 128, tb, ev, banks=qbanks)
                              if lastj:
                                  release_block()
                          jobs.append(job)
                  return jobs

              chk("GA" + str(l))
              for j in qk_jobs(0, (4, 5, 6, 7)):
                  j()
              chk("QK" + str(l), QK=QK[:], YA=YA[:])
              pend_te = []
              for h in range(4):
                  bg = qk_jobs(h + 1) if h < 3 else []
                  te_now = list(pend_te)
                  pend_te[:] = []
                  hb = h % 2
                  qorder = [7, 0, 6, 1, 5, 2, 4, 3]
                  items = [(pos, qb, kt) for pos, qb in enumerate(qorder) for kt in range(2 * qb + 2)]
                  n_items = len(items)
                  bg_every = max(1, n_items // (len(bg) + 1)) if bg else 0

                  def emit_S(i):
                      pos, qb, kt = items[i]
                      r = kt - 2 * qb
                      c0 = 128 if r == 1 else 0
                      sb = i % 3
                      Sb = PS[:, sb, :].rearrange("p (m n) -> p m n", m=2)

                      def fn():
                          for m in range(2):
                              ins = pe.matmul(out=Sb[:, m, c0:256],
                                              lhsT=QK[:, hb, 2, kt * 128:(kt + 1) * 128],
                                              rhs=QK[:, hb, m, qb * 256 + c0:(qb + 1) * 256],
                                              start=(m == 0), stop=(r < 0 and m == 1), skip_group_check=True)
                          if r >= 0:
                              ins = pe.matmul(out=PS[:, sb, r * 128:r * 128 + 320], lhsT=MA[0:1, :], rhs=MB[0:1, :],
                                              start=False, stop=True, skip_group_check=True)
                          return ins
                      S.op("pe", fn, reads=[f"qk{hb}_1_{kt // 4}", f"qk{hb}_0_{qb // 2}", "mska", "mskb"],
                           writes=[bankreg(sb)])

                  LA = 2
                  for i in range(min(LA, n_items)):
                      emit_S(i)
                  for i in range(n_items):
                      pos, qb, kt = items[i]
                      r = kt - 2 * qb
                      c0 = 128 if r == 1 else 0
                      sb = i % 3
                      pb_ = i % 4
                      Sb = PS[:, sb, :].rearrange("p (m n) -> p m n", m=2)
                      if i + LA < n_items:
                          emit_S(i + LA)
                      S.op("act", lambda: act.activation(out=PTB[:, pb_, :, c0:256], in_=Sb[:, :, c0:256], func=AF.Exp,
                                                         scale=0.125),
                           reads=[bankreg(sb)], writes=[f"pt{pb_}"])
                      ob = 3 + 2 * (pos % 2)
                      Ob = PS[:, ob:ob + 2, :].rearrange("p b (j c) -> p (b j) c", j=2)

                      def av():
                          ins = None
                          for m in range(2):
                              for j in range(2):
                                  if r == 1 and j == 0:
                                      continue
                                  a = m * 2 + j
                                  ins = pe.matmul(out=Ob[:, a, 0:130], lhsT=PTB[:, pb_, m, j * 128:(j + 1) * 128],
                                                  rhs=VA[:, kt, h, 0:130], start=(kt == 0 and j == 0),
                                                  stop=(kt == 2 * qb + j), skip_group_check=True)
                          return ins
                      S.op("pe", av, reads=[f"pt{pb_}", f"va{kt}", "va_ones"], writes=[bankreg(ob), bankreg(ob + 1)])
                      if kt == 2 * qb + 1:
                          rsb = RS[:, pos % 2, :]
                          S.op("dve", lambda: dve.reciprocal(out=rsb, in_=Ob[:, :, 128]),
                               reads=[bankreg(ob), bankreg(ob + 1)], writes=[f"rs{pos % 2}"])
                          S.op("dve", lambda: dve.tensor_scalar(out=rsb[:, 2:4], in0=rsb[:, 2:4], scalar1=NEGLAM,
                                                                scalar2=None, op0=ALU.mult),
                               reads=["neglam"], writes=[f"rs{pos % 2}"])
                          for j in range(2):
                              tt = 2 * qb + j
                              S.op("dve", lambda: dve.tensor_scalar(out=OUN[:, tt, :], in0=Ob[:, j, 0:128],
                                                                    scalar1=rsb[:, j:j + 1], scalar2=None, op0=ALU.mult),
                                   reads=[bankreg(ob), f"rs{pos % 2}"], writes=[f"oun{tt}"])
                              S.op("dve", lambda: dve.scalar_tensor_tensor(out=OUN[:, tt, :], in0=Ob[:, 2 + j, 0:128],
                                                                           scalar=rsb[:, 2 + j:3 + j], in1=OUN[:, tt, :],
                                                                           op0=ALU.mult, op1=ALU.add),
                                   reads=[bankreg(ob + 1), f"rs{pos % 2}"], writes=[f"oun{tt}"])
                              S.op("dve", lambda: dve.scalar_tensor_tensor(out=JO[:], in0=OUN[:, tt, :], scalar=1.0,
                                                                           in1=OUN[:, tt, :], op0=ALU.mult, op1=ALU.mult,
                                                                           accum_out=ST2[:, tt:tt + 1]),
                                   reads=[f"oun{tt}"], writes=["jo", f"oss{tt}"])
                      if bg and (i % bg_every == bg_every - 1):
                          bg.pop(0)()
                      if te_now and i in (3, 8, 13):
                          te_now.pop(0)()
                  while bg:
                      bg.pop(0)()
                  chk(f"H{h}_" + str(l))
                  def norm_job(h):
                      def job():
                          S.op("act", lambda: act.activation(out=ST2[:, 16:32], in_=ST2[:, 0:16], func=AF.Sqrt,
                                                             scale=1.0 / 128, bias=EPS),
                               reads=[f"oss{t}" for t in range(NT)], writes=["osq"])
                          S.op("dve", lambda: dve.reciprocal(out=ST2[:, 32:48], in_=ST2[:, 16:32]), reads=["osq"],
                               writes=["orstd"])
                          for t in range(NT):
                              S.op("dve", lambda t=t: dve.tensor_scalar(out=ONB[:, t, :], in0=OUN[:, t, :],
                                                                        scalar1=ST2[:, 32 + t:33 + t], scalar2=None,
                                                                        op0=ALU.mult),
                                   reads=[f"oun{t}", "orstd"], writes=[f"onb{t}"])
                      return job
                  def te_job(h, half):
                      def job():
                          pb = 7
                          PT_ = PS[:, pb, :].bitcast(BF16)

                          def tr():
                              for u in range(8):
                                  t = half * 8 + u
                                  ins = pe.transpose(out=PT_[:, u * 128:(u + 1) * 128], in_=ONB[:, t, :],
                                                     identity=IDENT[:])
                              return ins
                          S.op("pe", tr, reads=[f"onb{half * 8 + u}" for u in range(8)] + ["ident"],
                               writes=[bankreg(pb)])
                          S.op("dve", lambda: dve.scalar_tensor_tensor(
                              out=YA[:, h, half * 1024:(half + 1) * 1024], in0=PT_, scalar=SUBLN,
                              in1=YA[:, h, half * 1024:(half + 1) * 1024], op0=ALU.mult, op1=ALU.mult),
                              reads=[bankreg(pb), "subln_s"], writes=[f"ya{h}_{2 * half}", f"ya{h}_{2 * half + 1}"])
                      return job
                  pend_te[:] = [norm_job(h), te_job(h, 0), te_job(h, 1)]
                  if h == 3:
                      for j in pend_te:
                          j()
                      pend_te[:] = []
              open_stacks.pop().close()
          S.barrier()
          chk("B1" + str(l), YA=YA[:])

          with nc.sbuf_tensor(f"YB_{l}", [128, 4, T], BF16) as YB, nc.sbuf_tensor(f"YC_{l}", [128, 4, T], BF16) as YC:
              NP = 16 + 1040
              with nc.sbuf_tensor(f"PIN_{l}", [128, 2, NP], F32) as PIN, \
                      nc.sbuf_tensor(f"WA_{l}", [128, NP], F32) as WA, \
                      nc.sbuf_tensor(f"WB_{l}", [128, NP], F32) as WB, \
                      nc.sbuf_tensor(f"PL_{l}", [128, 2, 1024], BF16) as PL, \
                      nc.sbuf_tensor(f"T16_{l}", [128, 16], F32) as T16, \
                      nc.sbuf_tensor(f"TMPB_{l}", [128, 2, 512], BF16) as TMPB:
                  ALLB = (0, 1, 2, 3, 4, 5, 6, 7)
                  W, wregp = next_block("PIN")
                  Wp = W[:, 0:4096].rearrange("p (k n) -> p k n", k=8)
                  W, wregg = next_block("GB")
                  Wg = W[:, 0:4096].rearrange("p (k n) -> p k n", k=8)
                  S.op("dve", lambda: dve.memset(PIN[:, :, 0:16], 0.0), writes=["pin0", "pin1"])
                  S.op("dve", lambda: dve.memset(WA[:, 0:16], 0.0), writes=["wa"])
                  S.op("dve", lambda: dve.memset(WB[:, 0:16], 0.0), writes=["wb"])
                  units = [(g, hp) for g in (3, 2, 1, 0) for hp in range(2)]

                  def poolmm(g, hp):
                      for j in range(2):
                          tb = 2 * hp + j
                          b = (rr["b"]) % 8
                          rr["b"] += 1
                          S.op("pe", lambda: pe.matmul(out=bank(b), lhsT=PW[:, g, :], rhs=PL[:, hp, j * 512:(j + 1) * 512],
                                                       start=True, stop=True), reads=["pw", f"pl{hp}"], writes=[bankreg(b)])
                          tbf = tb % 2
                          S.op("act", lambda: act.activation(
                              out=TMPB[:, tbf, :], in_=bank(b), func=AF.Identity, scale=pp(l, 28 + g),
                              bias=ST[:, 10 + g:11 + g]), reads=[bankreg(b), "vpp", "bps"], writes=[f"tmpb{tbf}"])
                          S.op("pool", lambda: pool.tensor_tensor(
                              out=YB[:, g, tb * 512:(tb + 1) * 512], in0=YB[:, g, tb * 512:(tb + 1) * 512],
                              in1=TMPB[:, tbf, :], op=ALU.mult), reads=[f"tmpb{tbf}"], writes=[f"yb{g}_{tb}"])

                  prev = None
                  for ui, (g, hp) in enumerate(units):
                      h0 = 16 * hp
                      n = 1024 + h0
                      P_ = PIN[:, hp, :]
                      for j in range(2):
                          tb = 2 * hp + j

                          def ev(b, j=j):
                              c0 = 16 + h0 + j * 512
                              S.op("act", lambda: act.copy(out=P_[:, c0:c0 + 512], in_=bank(b)),
                                   reads=[bankreg(b)], writes=[f"pin{hp}"])
                              if hp == 0 and j == 1:
                                  S.op("act", lambda: act.copy(out=PIN[:, 1, 16:32], in_=PS[:, b, 496:512]),
                                       reads=[bankreg(b)], writes=["pin1"])
                          proj_fm(Wp, wregp, g * 128, tb, ev, banks=ALLB)
                      if ui == len(units) - 1:
                          release_block()
                      w = 2 ** (g + 1)
                      src, srcr = P_, f"pin{hp}"
                      bufs = [(WA, "wa"), (WB, "wb")]
                      sh = 1
                      bi = 0
                      while sh < w:
                          dst, dstr = bufs[bi % 2]
                          S.op("dve", lambda src=src, dst=dst, sh=sh: dve.tensor_tensor(
                              out=dst[:, 16:16 + n], in0=src[:, 16:16 + n], in1=src[:, 16 - sh:16 - sh + n], op=ALU.add),
                              reads=[srcr], writes=[dstr])
                          src, srcr = dst, dstr
                          sh *= 2
                          bi += 1
                      d0 = 16 + h0
                      S.op("dve", lambda src=src: dve.scalar_tensor_tensor(
                          out=PL[:, hp, :], in0=src[:, d0:d0 + 1024], scalar=1.0 / w, in1=P_[:, d0:d0 + 1024],
                          op0=ALU.mult, op1=ALU.subtract), reads=[srcr, f"pin{hp}"], writes=[f"pl{hp}"])
                      if hp == 0:
                          S.op("dve", lambda src=src: dve.tensor_tensor(out=T16[:], in0=src[:, 16:32], in1=RCNT[:, g, :],
                                                                        op=ALU.mult),
                               reads=[srcr, "rcnt"], writes=["t16"])
                          S.op("dve", lambda: dve.tensor_tensor(out=PL[:, 0, 0:16], in0=T16[:], in1=P_[:, 16:32],
                                                                op=ALU.subtract),
                               reads=["t16", "pin0"], writes=["pl0"])
                      for j in range(2):
                          tb = 2 * hp + j

                          def ev(b, tb=tb):
                              S.op("act", lambda: act.activation(out=YB[:, g, tb * 512:(tb + 1) * 512], in_=bank(b),
                                                                 func=AF.Silu),
                                   reads=[bankreg(b)], writes=[f"yb{g}_{tb}"])
                          proj_fm(Wg, wregg, g * 128, tb, ev, banks=ALLB)
                      if ui == len(units) - 1:
                          release_block()
                      if prev is not None:
                          poolmm(*prev)
                      prev = (g, hp)
                  poolmm(*prev)
              S.barrier()
              chk("B2" + str(l), YB=YB[:])
              with nc.sbuf_tensor(f"LNG_{l}", [128, 3, 512], F32) as LNG, \
                      nc.sbuf_tensor(f"GV_{l}", [128, NT, 512], BF16) as GV, \
                      nc.sbuf_tensor(f"TMPB2_{l}", [128, 2, 512], BF16) as TMPB, \
                      nc.sbuf_tensor(f"VN_{l}", [128, 1, 512], F32) as VN, \
                      nc.sbuf_tensor(f"VNB_{l}", [128, 2, 512], BF16) as VNB, \
                      nc.sbuf_tensor(f"BST_{l}", [128, NT, 6], F32) as BST, \
                      nc.sbuf_tensor(f"MV_{l}", [128, NT, 2], F32) as MV:
                  ALLB = (0, 1, 2, 3, 4, 5, 6, 7)
                  S.dma("sp", "lng0", LNG[:, 0, :], ln_g_d[:, l, :], writes=["lng0"])
                  S.dma("sp", "lng1", LNG[:, 1, :], ln_b_d[:, l, :], writes=["lng1"])
                  S.dma("sp", "lng2", LNG[:, 2, :], sgu_b_d[:, l, :], writes=["lng2"])
                  S.op("dve", lambda: dve.memset(SW[64:128, :, 0:64], 0.0), writes=["sw"])
                  W, wreg = next_block("SV")
                  Wv = W[:, 0:4096].rearrange("p (k n) -> p k n", k=8)
                  for t in range(NT):
                      def ev(b, t=t):
                          S.op("act", lambda: act.activation(out=GV[:, t, :], in_=bank(b), func=AF.Gelu),
                               reads=[bankreg(b)], writes=[f"gv{t}"])
                          S.op("dve", lambda: dve.bn_stats(out=BST[:, t, :], in_=GV[:, t, :]), reads=[f"gv{t}"],
                               writes=[f"bst{t}"])
                          S.op("dve", lambda: dve.bn_aggr(out=MV[:, t, :], in_=BST[:, t, :]), reads=[f"bst{t}"],
                               writes=[f"mv{t}"])
                      proj_tm(Wv, wreg, t, 512, ev, banks=ALLB)
                  release_block()
                  W, wregc = next_block("GC")
                  Wgc = W[:, 0:4096].rearrange("p (k n) -> p k n", k=8)
                  W, wregu = next_block("SU")
                  Wsu = W[:, 0:4096].rearrange("p (k n) -> p k n", k=8)

                  def gc_job(c, tb, rel):
                      def job():
                          def ev(b):
                              S.op("act", lambda: act.activation(out=YC[:, c, tb * 512:(tb + 1) * 512], in_=bank(b),
                                                                 func=AF.Silu),
                                   reads=[bankreg(b)], writes=[f"yc{c}_{tb}"])
                          proj_fm(Wgc, wregc, c * 128, tb, ev, banks=(4, 5, 6, 7))
                          if rel:
                              release_block()
                      return job

                  def su_job(c, tb, rel):
                      def job():
                          def ev(b):
                              tbf = rr["b"] % 2
                              S.op("act", lambda: act.activation(out=TMPB[:, tbf, :], in_=bank(b), func=AF.Gelu),
                                   reads=[bankreg(b)], writes=[f"tmpb{tbf}"])
                              S.op("pool", lambda: pool.tensor_tensor(
                                  out=YC[:, c, tb * 512:(tb + 1) * 512], in0=YC[:, c, tb * 512:(tb + 1) * 512],
                                  in1=TMPB[:, tbf, :], op=ALU.mult), reads=[f"tmpb{tbf}"], writes=[f"yc{c}_{tb}"])
                          proj_fm(Wsu, wregu, c * 128, tb, ev, banks=(4, 5, 6, 7))
                          if rel:
                              release_block()
                      return job
                  jobs = [gc_job(c, tb, rel=(tb == 3 and c == 3)) for tb in range(4) for c in range(4)]
                  jobs += [su_job(c, tb, rel=(tb == 3 and c == 3)) for tb in range(4) for c in range(4)]
                  for _ in range(8):
                      jobs.pop(0)()
                  S.op("act", lambda: act.activation(out=ST2[:, 32:48], in_=MV[:, :, 1], func=AF.Sqrt, scale=1.0,
                                                     bias=EPS), reads=[f"mv{t}" for t in range(NT)], writes=["ssq"])
                  S.op("dve", lambda: dve.reciprocal(out=ST2[:, 0:16], in_=ST2[:, 32:48]), reads=["ssq"],
                       writes=["srstd"])
                  S.op("dve", lambda: dve.scalar_tensor_tensor(out=ST2[:, 16:32], in0=MV[:, :, 0], scalar=-1.0,
                                                               in1=ST2[:, 0:16], op0=ALU.mult, op1=ALU.mult),
                       reads=["srstd"] + [f"mv{t}" for t in range(NT)], writes=["snmr"])
                  chk("SV" + str(l), GV=GV[:], MV=MV[:], ST2=ST2[:])

                  def norm(t):
                      vb = t % 2
                      S.op("act", lambda: act.activation(
                          out=VN[:, 0, :], in_=GV[:, t, :], func=AF.Identity, scale=ST2[:, t:t + 1],
                          bias=ST2[:, 16 + t:17 + t]), reads=[f"gv{t}", "srstd", "snmr"], writes=["vn0"])
                      S.op("dve", lambda: dve.tensor_tensor(out=VN[:, 0, :], in0=VN[:, 0, :], in1=LNG[:, 0, :],
                                                            op=ALU.mult), reads=["lng0"], writes=["vn0"])
                      S.op("dve", lambda: dve.tensor_tensor(out=VNB[:, vb, :], in0=VN[:, 0, :], in1=LNG[:, 1, :],
                                                            op=ALU.add), reads=["vn0", "lng1"],
                           writes=[f"vnb{vb}"])
                  norm(0)
                  for t in range(NT):
                      vb = t % 2
                      if t + 1 < NT:
                          norm(t + 1)
                      b = t % 4

                      def mix(vb=vb, b=b):
                          for g in range(4):
                              ins = pe.matmul(out=PS[:, b, g * 128:(g + 1) * 128], lhsT=VNB[:, vb, g * 128:(g + 1) * 128],
                                              rhs=SW[:, g, :], start=True, stop=True, skip_group_check=True)
                          return ins
                      S.op("pe", mix, reads=[f"vnb{vb}", "sw"], writes=[bankreg(b)])
                      S.op("dve", lambda b=b, t=t: dve.tensor_tensor(out=GV[:, t, :], in0=bank(b), in1=LNG[:, 2, :],
                                                                     op=ALU.add),
                           reads=[bankreg(b), "lng2"], writes=[f"gv{t}"])
                      S.op("dve", lambda t=t: dve.tensor_tensor(
                          out=YC[:, :, t * 128:(t + 1) * 128], in0=YC[:, :, t * 128:(t + 1) * 128],
                          in1=GV[:, t, :].rearrange("p (g i) -> p g i", g=4), op=ALU.mult),
                          reads=[f"gv{t}"], writes=[f"yc{c}_{t // 4}" for c in range(4)])
                      for _ in range(2 if t % 2 == 0 else 1):
                          if jobs:
                              jobs.pop(0)()
                  while jobs:
                      jobs.pop(0)()
              S.barrier()
              chk("B3" + str(l), YC=YC[:])
              with nc.sbuf_tensor(f"MG_{l}", [128, 8, 1024], BF16) as MG, \
                      nc.sbuf_tensor(f"GT_{l}", [128, 1, 3, 512], BF16) as GT, \
                      nc.sbuf_tensor(f"M0_{l}", [128, 1, 512], F32) as M0, \
                      nc.sbuf_tensor(f"M1_{l}", [128, 1, 512], F32) as M1, \
                      nc.sbuf_tensor(f"OT_{l}", [128, 2, 512], F32) as OT, \
                      nc.sbuf_tensor(f"JC_{l}", [128, 512], BF16) as JC:
                  Ys = [(YA, "ya"), (YB, "yb"), (YC, "yc")]
                  it = 0
                  for hf in range(2):
                      for c in range(8):
                          W, wreg = next_block(f"C{hf}_{c}")
                          for tbl in range(2):
                              tb = 2 * hf + tbl
                              gi = 0
                              it += 1
                              gb = []
                              for n in range(3):
                                  b = rr["b"] % 8
                                  rr["b"] += 1

                                  def fn(n=n, b=b, tb=tb, W=W):
                                      for k in range(8):
                                          ins = pe.matmul(out=bank(b), lhsT=W[:, (n * 8 + k) * 128:(n * 8 + k + 1) * 128],
                                                          rhs=HTb(k, tb), start=(k == 0), stop=(k == 7))
                                      return ins
                                  S.op("pe", fn, reads=[wreg, f"ht{tb}"], writes=[bankreg(b)])
                                  S.op("act", lambda n=n, b=b, gi=gi: act.activation(
                                      out=GT[:, gi, n, :], in_=bank(b), func=AF.Sigmoid, bias=pp(l, n * 8 + c)),
                                      reads=[bankreg(b), "vpp"], writes=[f"gt{gi}_{n}"])
                              bb = []
                              for n in range(3):
                                  b = rr["b"] % 8
                                  rr["b"] += 1
                                  bb.append(b)
                                  Yn, yname = Ys[n]

                                  def fn(n=n, b=b, tb=tb, W=W, Yn=Yn):
                                      for k in range(4):
                                          o = 3072 + (n * 4 + k) * 128
                                          ins = pe.matmul(out=bank(b), lhsT=W[:, o:o + 128],
                                                          rhs=Yn[:, k, tb * 512:(tb + 1) * 512],
                                                          start=(k == 0), stop=(k == 3))
                                      return ins
                                  S.op("pe", fn, reads=[wreg] + [f"{yname}{k}_{tb}" for k in range(4)],
                                       writes=[bankreg(b)])
                              S.op("dve", lambda gi=gi, b=bb[0]: dve.tensor_tensor(out=M0[:, gi, :], in0=bank(b),
                                                                                   in1=GT[:, gi, 0, :], op=ALU.mult),
                                   reads=[bankreg(bb[0]), f"gt{gi}_0"], writes=[f"m0_{gi}"])
                              S.op("dve", lambda gi=gi, b=bb[1]: dve.tensor_tensor(out=M1[:, gi, :], in0=bank(b),
                                                                                   in1=GT[:, gi, 1, :], op=ALU.mult),
                                   reads=[bankreg(bb[1]), f"gt{gi}_1"], writes=[f"m1_{gi}"])
                              S.op("pool", lambda gi=gi: pool.tensor_tensor(out=M0[:, gi, :], in0=M0[:, gi, :],
                                                                            in1=M1[:, gi, :], op=ALU.add),
                                   reads=[f"m1_{gi}"], writes=[f"m0_{gi}"])
                              S.op("dve", lambda gi=gi, b=bb[2]: dve.tensor_tensor(out=M1[:, gi, :], in0=bank(b),
                                                                                   in1=GT[:, gi, 2, :], op=ALU.mult),
                                   reads=[bankreg(bb[2]), f"gt{gi}_2"], writes=[f"m1_{gi}"])
                              S.op("pool", lambda gi=gi, c=c, tbl=tbl: pool.tensor_tensor(
                                  out=MG[:, c, tbl * 512:(tbl + 1) * 512], in0=M0[:, gi, :], in1=M1[:, gi, :],
                                  op=ALU.add), reads=[f"m0_{gi}", f"m1_{gi}"], writes=[f"mg{c}_{tbl}"])
                          release_block()
                      W0, wreg0 = next_block(f"WO{hf}_0")
                      W1, wreg1 = next_block(f"WO{hf}_1")
                      Wo = [W0[:, 0:4096].rearrange("p (k n) -> p k n", k=8),
                            W1[:, 0:4096].rearrange("p (k n) -> p k n", k=8)]
                      wor = [wreg0, wreg1]
                      JX = M1[:, 0, :].bitcast(BF16)

                      def ssq_next(tt):
                          S.op("act", lambda: act.activation(out=JX, in_=X[:, tt, :], func=AF.Square,
                                                             accum_out=ST[:, 48 + tt:49 + tt]),
                               reads=[f"x{tt}"], writes=["m1_0", f"ssn{tt}"])
                      for tl in range(8):
                          t = hf * 8 + tl
                          obs = []
                          for hc in range(2):
                              b = rr["b"] % 8
                              rr["b"] += 1
                              obs.append(b)

                              def fn(hc=hc, b=b, tl=tl):
                                  for k in range(8):
                                      ins = pe.matmul(out=bank(b), lhsT=MG[:, k, tl * 128:(tl + 1) * 128],
                                                      rhs=Wo[hc][:, k, :], start=(k == 0), stop=(k == 7))
                                  return ins
                              S.op("pe", fn, reads=[wor[hc]] + [f"mg{k}_{tl // 4}" for k in range(8)],
                                   writes=[bankreg(b)])
                              S.op("act", lambda hc=hc, b=b, t=t: act.activation(
                                  out=JC[:], in_=bank(b), func=AF.Square, accum_out=ST[:, 16 + 2 * t + hc:17 + 2 * t + hc]),
                                  reads=[bankreg(b)], writes=["jc", f"pss{t}_{hc}"])
                          S.op("dve", lambda t=t: dve.tensor_tensor(out=ST2[:, t:t + 1], in0=ST[:, 16 + 2 * t:17 + 2 * t],
                                                                    in1=ST[:, 17 + 2 * t:18 + 2 * t], op=ALU.add),
                               reads=[f"pss{t}_0", f"pss{t}_1"], writes=[f"pssum{t}"])
                          S.op("act", lambda t=t: act.activation(out=ST2[:, 16 + t:17 + t], in_=ST2[:, t:t + 1],
                                                                 func=AF.Sqrt, scale=1.0 / D, bias=EPS),
                               reads=[f"pssum{t}"], writes=[f"psq{t}"])
                          S.op("dve", lambda t=t: dve.reciprocal(out=ST2[:, 32 + t:33 + t], in_=ST2[:, 16 + t:17 + t]),
                               reads=[f"psq{t}"], writes=[f"prstd{t}"])
                          for hc in range(2):
                              b = obs[hc]
                              ob_ = (2 * tl + hc) % 2
                              S.op("dve", lambda hc=hc, b=b, t=t, ob_=ob_: dve.scalar_tensor_tensor(
                                  out=OT[:, ob_, :], in0=bank(b), scalar=ST2[:, 32 + t:33 + t],
                                  in1=GREP[:, hc * 512:(hc + 1) * 512], op0=ALU.mult, op1=ALU.mult),
                                  reads=[bankreg(b), f"prstd{t}", "grep"], writes=[f"ot{ob_}"])
                              eng_ = "pool" if hc == 0 else "dve"
                              E_ = pool if hc == 0 else dve
                              S.op(eng_, lambda hc=hc, t=t, ob_=ob_, E_=E_: E_.tensor_tensor(
                                  out=X[:, t, hc * 512:(hc + 1) * 512], in0=X[:, t, hc * 512:(hc + 1) * 512],
                                  in1=OT[:, ob_, :], op=ALU.add), reads=[f"ot{ob_}"], writes=[f"x{t}"])
                          if last:
                              S.dma("sp", "out", y_d[t * 128:(t + 1) * 128, :], X[:, t, :], reads=[f"x{t}"])
                          elif tl >= 1:
                              ssq_next(t - 1)
                      if not last:
                          ssq_next(hf * 8 + 7)
                      release_block()
                      release_block()
              S.barrier()
              chk("C" + str(l))
    except _Stop:
        S.barrier(hard=True)
        for t in range(NT):
            S.dma("sp", "out", y_d[t * 128:(t + 1) * 128, :], X[:, t, :], reads=[f"x{t}"])
    nc.sync.wait_ge(S.dsem["out"], S.dcnt["out"])
    return nc


def _prep_weights(inp):
    f = np.float32
    w_in = np.asarray(inp["w_in"], f)
    cols = list(range(1024, 1536)) + list(range(1536, 2048))
    for h in range(4):
        cols += list(range(h * 128, (h + 1) * 128)) + list(range(512 + h * 128, 512 + (h + 1) * 128))
    cols += list(range(2560, 3072)) + list(range(2048, 2560)) + list(range(4096, 4608)) + \
        list(range(3072, 3584)) + list(range(3584, 4096))
    cols = np.array(cols)
    w_in_p = np.ascontiguousarray(w_in[:, :, cols].reshape(L, 8, 128, 4608).transpose(0, 2, 1, 3))
    w_merge = np.asarray(inp["w_merge"], f)
    w_branch = np.asarray(inp["w_branch"], f)
    wm = w_merge.reshape(L, 8, 128, 3, 8, 128)
    wm = wm.transpose(0, 4, 2, 3, 1, 5).reshape(L, 8, 128, 3072)
    wb = w_branch.reshape(L, 3, 4, 128, 8, 128)
    wb = wb.transpose(0, 4, 3, 1, 2, 5).reshape(L, 8, 128, 1536)
    w_c = np.ascontiguousarray(np.concatenate([wm, wb], axis=-1))
    w_out = np.asarray(inp["w_out"], f)
    w_out_p = np.ascontiguousarray(w_out.reshape(L, 8, 128, 1024).transpose(0, 2, 1, 3))
    pool_w_p = np.ascontiguousarray(np.asarray(inp["pool_w"], f).transpose(0, 2, 1, 3))
    sgu_wT = np.ascontiguousarray(np.asarray(inp["sgu_w"], f).transpose(0, 3, 1, 2))
    vec = np.zeros((128, L * NPP), f)
    for l in range(L):
        vec[:, l * NPP:l * NPP + 24] = np.asarray(inp["b_merge"], f)[l].reshape(24, 128).T
        vec[:, l * NPP + 24:l * NPP + 28] = np.asarray(inp["pool_b"], f)[l].T
        vec[:, l * NPP + 28:l * NPP + 32] = np.asarray(inp["pool_scale"], f)[l].reshape(4, 128).T
        vec[:, l * NPP + 32] = np.asarray(inp["attn_subln_g"], f)[l]

    def rep(a):
        a = np.asarray(a, f)
        return np.ascontiguousarray(np.broadcast_to(a[None], (128,) + a.shape))
    lam = np.stack([np.asarray(inp[k], f) for k in ("lambda_q1", "lambda_k1", "lambda_q2", "lambda_k2")], axis=1)
    rc = np.zeros((4, 16), f)
    for g in range(4):
        w = 2 ** (g + 1)
        for t in range(16):
            rc[g, t] = 1.0 / min(t + 1, w)
    return {
        "w_in_p": w_in_p, "w_c": w_c, "w_out_p": w_out_p, "pool_w_p": pool_w_p, "sgu_wT": sgu_wT,
        "vec_pp": vec, "pre_g_rep": rep(inp["pre_norm_g"]), "post_g_rep": rep(inp["post_norm_g"]),
        "ln_g_rep": rep(inp["sgu_ln_g"]), "ln_b_rep": rep(inp["sgu_ln_b"]),
        "sgu_b_rep": rep(np.asarray(inp["sgu_b"], f).reshape(L, 512)), "lam_rep": rep(lam),
        "ident": np.eye(128, dtype=f), "rcnt": rep(rc),
        "mask_a": (np.arange(128) >= 64).astype(f)[None, :], "mask_b": np.concatenate([np.full((1, 64), -30000.0, f), np.zeros((1, 192), f), np.full((1, 64), -30000.0, f)], axis=1),
    }


def kernel(**inputs):
    x = np.asarray(inputs["x"], np.float32)
    B = x.shape[0]
    shared = _prep_weights(inputs)
    nc = build()
    in_maps = []
    for b in range(B):
        m = dict(shared)
        m["x"] = np.ascontiguousarray(x[b])
        in_maps.append(m)
    res = run_bass_kernel_spmd(nc, in_maps, core_ids=list(range(B)))
    return np.stack([np.asarray(r["y"], np.float32) for r in res.results], axis=0)
```
